# Optimizing a Trainium2 kernel written in Bass

```python
import jax, jax.numpy as jnp
from jax import lax
import numpy as np

D_MODEL = 2048
BATCH = 4
SEQ = 4096
DEPTH = 4

N_EVEN = (DEPTH + 1) // 2
N_ODD = DEPTH // 2
EPS = 1e-6
D_FF = 4 * D_MODEL
GROUP_DIM = 128
D_A = D_MODEL // 2
D_B = D_MODEL - D_A
N_A_GROUPS = D_A // GROUP_DIM
CHUNK = 128
CONV_WIDTH = 31
D_AB_IN = 2 * D_A + 2 * D_B
N_HEADS = 16
Q_RANK = 512
KV_RANK = 512
NOPE_DIM = 128
ROPE_DIM = 64
V_DIM = 128
QK_DIM = NOPE_DIM + ROPE_DIM
D_MLA_IN = Q_RANK + KV_RANK + ROPE_DIM
ROPE_THETA = 10000.0
Q_BLOCK = 128
ATTN_SCALE = QK_DIM ** -0.5

kernel_name = "hybrid_sgu_conv_mla_adaln_trunk"


def _rms(x, g):
    xf = x.astype(jnp.float32)
    y = xf * lax.rsqrt(jnp.mean(xf * xf, axis=-1, keepdims=True) + EPS)
    return (y * g.astype(jnp.float32)).astype(x.dtype)


def _layernorm(x, g, b):
    xf = x.astype(jnp.float32)
    mu = jnp.mean(xf, axis=-1, keepdims=True)
    var = jnp.mean(jnp.square(xf - mu), axis=-1, keepdims=True)
    y = (xf - mu) * lax.rsqrt(var + EPS)
    return (y * g.astype(jnp.float32) + b.astype(jnp.float32)).astype(x.dtype)


def _modulate(h, shift, scale):
    return h * (1 + scale[:, None, :]) + shift[:, None, :]


def _spatial_gating(u, v, v_norm_g, w_s, b_s):
    bsz, s, _ = u.shape
    shp = (bsz, s // CHUNK, CHUNK, N_A_GROUPS, GROUP_DIM)
    v = _rms(v.reshape(shp), v_norm_g)
    mask = jnp.tril(jnp.ones((CHUNK, CHUNK), dtype=w_s.dtype))
    mixed = jnp.einsum('gts,bnsgd->bntgd', w_s * mask, v) + b_s.T[None, None, :, :, None]
    return (u.reshape(shp) * mixed).reshape(bsz, s, D_A)


def _conformer_conv(a, g, conv_w, conv_b, ln_g, ln_b):
    y = a * jax.nn.sigmoid(g)
    y = lax.conv_general_dilated(y, conv_w[:, None, :], window_strides=(1,),
                                 padding=[(CONV_WIDTH - 1, 0)],
                                 dimension_numbers=('NWC', 'WIO', 'NWC'),
                                 feature_group_count=D_B) + conv_b
    return jax.nn.silu(_layernorm(y, ln_g, ln_b))


def _even_mixer(h, w_in, sgu_norm_g, sgu_w, sgu_b, conv_w, conv_b, ln_g, ln_b, w_out):
    proj = h @ w_in
    u, v, a, g = jnp.split(proj, [D_A, 2 * D_A, 2 * D_A + D_B], axis=-1)
    out_a = _spatial_gating(jax.nn.gelu(u), jax.nn.gelu(v), sgu_norm_g, sgu_w, sgu_b)
    out_b = _conformer_conv(a, g, conv_w, conv_b, ln_g, ln_b)
    return jnp.concatenate([out_a, out_b], axis=-1) @ w_out


def _rope_tables(s):
    pos = jnp.arange(s, dtype=jnp.float32)
    inv = ROPE_THETA ** (-jnp.arange(0, ROPE_DIM, 2, dtype=jnp.float32) / ROPE_DIM)
    ang = pos[:, None] * inv[None, :]
    return jnp.cos(ang), jnp.sin(ang)


def _apply_rope(x, cos, sin):
    xf = x.astype(jnp.float32)
    x1, x2 = jnp.split(xf, 2, axis=-1)
    return jnp.concatenate([x1 * cos - x2 * sin, x1 * sin + x2 * cos], axis=-1).astype(x.dtype)


def _segment_head_norm(t, g):
    return jnp.concatenate([_rms(t[..., :NOPE_DIM], g[:NOPE_DIM]),
                            _rms(t[..., NOPE_DIM:], g[NOPE_DIM:])], axis=-1)


def _block_causal_attention(q_nope, q_rope, k_nope, k_rope, v):
    s = q_nope.shape[1]
    outs = []
    for i in range(s // Q_BLOCK):
        q0, q1 = i * Q_BLOCK, (i + 1) * Q_BLOCK
        sc = (jnp.einsum('bqhd,bkhd->bhqk', q_nope[:, q0:q1], k_nope[:, :q1])
              + jnp.einsum('bqhr,bkr->bhqk', q_rope[:, q0:q1], k_rope[:, :q1]))
        sc = sc.astype(jnp.float32) * ATTN_SCALE
        qi = q0 + jnp.arange(Q_BLOCK)
        ki = jnp.arange(q1)
        sc = jnp.where(ki[None, :] <= qi[:, None], sc, -jnp.inf)
        p = jax.nn.softmax(sc, axis=-1).astype(v.dtype)
        outs.append(jnp.einsum('bhqk,bkhd->bqhd', p, v[:, :q1]))
    return jnp.concatenate(outs, axis=1)


def _mla_mixer(h, w_in, q_norm_g, kv_norm_g, w_uq, w_ukv, q_head_g, k_head_g, w_out):
    bsz, s, _ = h.shape
    proj = h @ w_in
    c_q, c_kv, k_rope = jnp.split(proj, [Q_RANK, Q_RANK + KV_RANK], axis=-1)
    q = (_rms(c_q, q_norm_g) @ w_uq).reshape(bsz, s, N_HEADS, QK_DIM)
    kv = (_rms(c_kv, kv_norm_g) @ w_ukv).reshape(bsz, s, N_HEADS, NOPE_DIM + V_DIM)
    k_nope, v = jnp.split(kv, [NOPE_DIM], axis=-1)
    q = _segment_head_norm(q, q_head_g)
    k_nope = _rms(k_nope, k_head_g[:NOPE_DIM])
    k_rope = _rms(k_rope, k_head_g[NOPE_DIM:])
    cos, sin = _rope_tables(s)
    q_rope = _apply_rope(q[..., NOPE_DIM:], cos[:, None, :], sin[:, None, :])
    k_rope = _apply_rope(k_rope, cos, sin)
    o = _block_causal_attention(q[..., :NOPE_DIM], q_rope, k_nope, k_rope, v)
    return o.reshape(bsz, s, N_HEADS * V_DIM) @ w_out


def setup_inputs(seed: int = 0) -> dict:
    key = jax.random.key(seed)
    ks = jax.random.split(key, 32)
    f = jnp.float32
    nrm = lambda k, shp, sc: jax.random.normal(k, shp, f) * sc
    d = D_MODEL
    return {
        "x": nrm(ks[0], (BATCH, SEQ, d), 1.0),
        "c": nrm(ks[1], (BATCH, d), 1.0),
        "norm1_g": 1.0 + nrm(ks[2], (DEPTH, d), 0.02),
        "norm2_g": 1.0 + nrm(ks[3], (DEPTH, d), 0.02),
        "ada_w": nrm(ks[4], (DEPTH, d, 6 * d), 0.5 * d ** -0.5),
        "ada_b": nrm(ks[5], (DEPTH, 6 * d), 0.01),
        "mlp_w1": nrm(ks[6], (DEPTH, d, D_FF), d ** -0.5),
        "mlp_w2": nrm(ks[7], (DEPTH, D_FF, d), D_FF ** -0.5),
        "ab_w_in": nrm(ks[8], (N_EVEN, d, D_AB_IN), d ** -0.5),
        "sgu_norm_g": 1.0 + nrm(ks[9], (N_EVEN, N_A_GROUPS, GROUP_DIM), 0.02),
        "sgu_w": nrm(ks[10], (N_EVEN, N_A_GROUPS, CHUNK, CHUNK), CHUNK ** -0.5),
        "sgu_b": 1.0 + nrm(ks[11], (N_EVEN, N_A_GROUPS, CHUNK), 0.02),
        "conv_w": nrm(ks[12], (N_EVEN, CONV_WIDTH, D_B), CONV_WIDTH ** -0.5),
        "conv_b": nrm(ks[13], (N_EVEN, D_B), 0.01),
        "conv_ln_g": 1.0 + nrm(ks[14], (N_EVEN, D_B), 0.02),
        "conv_ln_b": nrm(ks[15], (N_EVEN, D_B), 0.01),
        "ab_w_out": nrm(ks[16], (N_EVEN, D_A + D_B, d), (D_A + D_B) ** -0.5),
        "mla_w_in": nrm(ks[17], (N_ODD, d, D_MLA_IN), d ** -0.5),
        "mla_q_norm_g": 1.0 + nrm(ks[18], (N_ODD, Q_RANK), 0.02),
        "mla_kv_norm_g": 1.0 + nrm(ks[19], (N_ODD, KV_RANK), 0.02),
        "mla_w_uq": nrm(ks[20], (N_ODD, Q_RANK, N_HEADS * QK_DIM), Q_RANK ** -0.5),
        "mla_w_ukv": nrm(ks[21], (N_ODD, KV_RANK, N_HEADS * (NOPE_DIM + V_DIM)), KV_RANK ** -0.5),
        "mla_q_head_g": 1.0 + nrm(ks[22], (N_ODD, QK_DIM), 0.02),
        "mla_k_head_g": 1.0 + nrm(ks[23], (N_ODD, QK_DIM), 0.02),
        "mla_w_out": nrm(ks[24], (N_ODD, N_HEADS * V_DIM, d), (N_HEADS * V_DIM) ** -0.5),
    }


def reference(x, c, norm1_g, norm2_g, ada_w, ada_b, mlp_w1, mlp_w2,
              ab_w_in, sgu_norm_g, sgu_w, sgu_b, conv_w, conv_b, conv_ln_g, conv_ln_b, ab_w_out,
              mla_w_in, mla_q_norm_g, mla_kv_norm_g, mla_w_uq, mla_w_ukv,
              mla_q_head_g, mla_k_head_g, mla_w_out):
    c_act = jax.nn.silu(c.astype(x.dtype))
    for l in range(DEPTH):
        mod = c_act @ ada_w[l] + ada_b[l]
        shift1, scale1, gate1, shift2, scale2, gate2 = jnp.split(mod, 6, axis=-1)
        h = _modulate(_rms(x, norm1_g[l]), shift1, scale1)
        if l % 2 == 0:
            e = l // 2
            mix = _even_mixer(h, ab_w_in[e], sgu_norm_g[e], sgu_w[e], sgu_b[e], conv_w[e],
                              conv_b[e], conv_ln_g[e], conv_ln_b[e], ab_w_out[e])
        else:
            o = l // 2
            mix = _mla_mixer(h, mla_w_in[o], mla_q_norm_g[o], mla_kv_norm_g[o], mla_w_uq[o],
                             mla_w_ukv[o], mla_q_head_g[o], mla_k_head_g[o], mla_w_out[o])
        x = x + gate1[:, None, :] * mix
        h = _modulate(_rms(x, norm2_g[l]), shift2, scale2)
        x = x + gate2[:, None, :] * (jnp.square(jax.nn.relu(h @ mlp_w1[l])) @ mlp_w2[l])
    return x
```

```python
import numpy as np
from contextlib import ExitStack
import concourse.bass as bass
import concourse.mybir as mybir
from concourse.bass_utils import run_bass_kernel_spmd

F32 = mybir.dt.float32
BF16 = mybir.dt.bfloat16
AF = mybir.ActivationFunctionType
ALU = mybir.AluOpType

P = 128
D = 2048
NCH = 16
TOK = 2048
SEQ = 4096
DFF = 8192
EPS = 1e-6
TM = 256
TF = 512
NCORES = 8
PAIRS = [[0, 1], [2, 3], [4, 5], [6, 7]]
ATTN_SCALE = 192 ** -0.5
SLOT_ELEMS = 4096
NSLOT = 3
import os
STAGE = int(os.environ.get('KSTAGE', '9'))
SUB = int(os.environ.get('KSUB', '9'))
GELU_F = AF.Identity if os.environ.get('KGELU') == '0' else AF.Gelu_apprx_tanh


class Buf:
    __slots__ = ("name", "w", "r", "dsem", "dcnt")

    def __init__(self, name):
        self.name = name
        self.w = None
        self.r = {}
        self.dsem = None
        self.dcnt = 0


class Eng:
    def __init__(self, name, sem):
        self.name = name
        self.sem = sem
        self.cnt = 0
        self.prog = []
        self.known = {}


class Sched:
    def __init__(self, nc, stack):
        self.nc = nc
        self.stack = stack
        self.nsem = 0
        self.sem_pool = []
        self.E = {n: Eng(n, self.new_sem("p_" + n)) for n in ("pe", "act", "dve", "pool", "sp")}

    def new_sem(self, name):
        self.nsem += 1
        return self.stack.enter_context(self.nc.semaphore(name + "_%d" % self.nsem))

    def _collect(self, E, reads, writes):
        need = {}

        def add(tok):
            if tok is None:
                return
            k = id(tok[0])
            if k not in need or need[k][1] < tok[1]:
                need[k] = tok

        for b in reads:
            add(b.w)
        for b in writes:
            add(b.w)
            for t in b.r.values():
                add(t)
        out = []
        for k, (sem, val) in need.items():
            if E.name == "pe" and sem is E.sem:
                continue
            if E.known.get(k, 0) >= val:
                continue
            E.known[k] = val
            out.append((sem, val))
        return out

    def op(self, eng, emit, reads=(), writes=()):
        E = self.E[eng]
        waits = self._collect(E, reads, writes)
        E.cnt += 1
        tok = (E.sem, E.cnt)
        E.prog.append((waits, emit, (E.sem, 1)))
        for b in reads:
            b.r[id(E.sem)] = tok
        for b in writes:
            b.w = tok
            b.r = {}

    def dma(self, q, emit, reads=(), writes=(), track=None, inc=16):
        E = self.E[q]
        waits = self._collect(E, reads, writes)
        tb = track
        if tb.dsem is None:
            if self.sem_pool:
                tb.dsem, tb.dcnt = self.sem_pool.pop()
            else:
                tb.dsem = self.new_sem("d")
        tb.dcnt += inc
        tok = (tb.dsem, tb.dcnt)
        E.prog.append((waits, emit, (tb.dsem, inc)))
        for b in reads:
            b.r[id(tb.dsem)] = tok
        for b in writes:
            b.w = tok
            b.r = {}

    def retire(self, bufs):
        for b in bufs:
            if b.dsem is not None:
                self.sem_pool.append((b.dsem, b.dcnt))
                b.dsem = None

    def fresh_progress(self):
        names = ("pe", "act", "dve", "pool", "sp")
        finals = [(self.E[n].sem, self.E[n].cnt) for n in names if self.E[n].cnt > 0]
        for n in names:
            E = self.E[n]
            waits = []
            for sem, val in finals:
                if sem is E.sem:
                    continue
                if E.known.get(id(sem), 0) >= val:
                    continue
                E.known[id(sem)] = val
                waits.append((sem, val))
            if waits:
                E.prog.append((waits, None, None))
        for n in names:
            E = self.E[n]
            E.sem = self.new_sem("p_" + n)
            E.cnt = 0

    def barrier(self, bufs, engines):
        for en in engines:
            E = self.E[en]
            waits = self._collect(E, (), bufs)
            if waits:
                E.prog.append((waits, None, None))

    def emit_all(self, block):
        def run(E, e):
            for waits, emit, inc in E.prog:
                for sem, val in waits:
                    e.wait_ge(sem, val)
                if emit is not None:
                    ins = emit(e)
                    ins.then_inc(inc[0], inc[1])

        @block.tensor
        def _(e):
            run(self.E["pe"], e)

        @block.scalar
        def _(e):
            run(self.E["act"], e)

        @block.vector
        def _(e):
            run(self.E["dve"], e)

        @block.gpsimd
        def _(e):
            run(self.E["pool"], e)

        @block.sync
        def _(e):
            run(self.E["sp"], e)


class Arena:
    def __init__(self, t, nelem):
        self.t = t
        self.n = nelem
        self.off = 0

    def reset(self):
        self.off = 0

    def alloc(self, free_shape, dt):
        n = int(np.prod(free_shape))
        ne = n * (2 if dt == F32 else 1)
        ne = (ne + 15) // 16 * 16
        assert self.off + ne <= self.n, ("arena overflow", self.off, ne, self.n)
        ap = self.t[:, self.off:self.off + ne]
        self.off += ne
        if dt == F32:
            ap = ap.bitcast(F32)
        ap = ap[:, 0:n]
        if len(free_shape) == 2:
            ap = ap.rearrange("p (a b) -> p a b", a=free_shape[0])
        elif len(free_shape) == 3:
            ap = ap.rearrange("p (a b c) -> p a b c", a=free_shape[0], b=free_shape[1])
        return ap


def build_program(layers):
    nc = bass.Bass("TRN2", target_bir_lowering=False)
    stack = ExitStack()
    dr = {}

    def din(name, shape, dt=F32):
        dr[name] = nc.dram_tensor(name, list(shape), dt, kind="ExternalInput").ap()
        return dr[name]

    xT_d = din("xT", [D, TOK])
    cT_d = din("cT", [P, 64])
    adaw_d = din("adaw", [4, D, 1536])
    adab_d = din("adab", [4, 4, 1536])
    onehot_d = din("onehot", [4, 1])
    n1g_d = din("n1g", [P, 64])
    n2g_d = din("n2g", [P, 64])
    lay_mlp = [l for l in layers] if STAGE >= 4 else []
    lay_even = [l for l in layers if l % 2 == 0] if STAGE >= 2 else []
    lay_odd = [l for l in layers if l % 2 == 1] if STAGE >= 2 else []
    w1_d = din("w1", [len(lay_mlp), D, DFF] if lay_mlp else [1, 1, 1])
    w2_d = din("w2", [len(lay_mlp), DFF, D] if lay_mlp else [1, 1, 1])
    abin_d = din("abin", [len(lay_even), D, 4096] if lay_even else [1, 1, 1])
    about_d = din("about", [len(lay_even), D, D] if lay_even else [1, 1, 1])
    wmT_d = din("wmT", [P, 2 * 1024])
    sgug_d = din("sgug", [P, 2 * 1024])
    sgub_d = din("sgub", [P, 2 * 1024])
    convw_d = din("convw", [P, 2 * 8 * 31])
    convb_d = din("convb", [P, 16])
    lng_d = din("lng", [P, 16])
    lnb_d = din("lnb", [P, 16])
    mlain_d = din("mlain", [len(lay_odd), D, 1152] if lay_odd else [1, 1, 1])
    uq_d = din("uq", [len(lay_odd), 512, 4096] if lay_odd else [1, 1, 1])
    ukv_d = din("ukv", [len(lay_odd), 512, 4096] if lay_odd else [1, 1, 1])
    mlaout_d = din("mlaout", [len(lay_odd), D, D] if lay_odd else [1, 1, 1])
    qng_d = din("qng", [P, 8])
    kvng_d = din("kvng", [P, 8])
    qhgn_d = din("qhgn", [P, 2])
    qhgr_d = din("qhgr", [64, 4])
    khgn_d = din("khgn", [P, 2])
    khgr_d = din("khgr", [64, 4])
    cos2_d = din("cos2", [64, TOK])
    sin2_d = din("sin2s", [64, TOK])
    tri_d = din("tri", [P, P])
    trineg_d = din("trineg", [P, P])
    rbias_d = din("rbias", [P, 1])
    cmask_d = din("cmask", [P, 1])
    outT_d = nc.dram_tensor("outT", [D, TOK], F32, kind="ExternalOutput").ap()

    modib = nc.dram_tensor("modib", [4, 6144], F32)
    modob = nc.dram_tensor("modob", [32, 6144], F32)
    haloib = {e: nc.dram_tensor("haloib%d" % e, [1024, 32], F32) for e in range(2)}
    haloob = {e: nc.dram_tensor("haloob%d" % e, [2048, 32], F32) for e in range(2)}
    kvib = {(o, t): nc.dram_tensor("kvib%d_%d" % (o, t), [576, TM], BF16) for o in range(2) for t in range(8)}
    kvob = {(o, t): nc.dram_tensor("kvob%d_%d" % (o, t), [1152, TM], BF16) for o in range(2) for t in range(8)}
    latloc = {o: nc.dram_tensor("latloc%d" % o, [576, TOK], BF16) for o in range(2)}
    cqscr = {o: nc.dram_tensor("cqscr%d" % o, [512, TOK], BF16) for o in range(2)}

    S = Sched(nc, stack)
    sb = lambda name, shape, dt: stack.enter_context(nc.sbuf_tensor(name, shape, dt))

    def bfcopy(name, src):
        return nc.dram_tensor(name + "_bf", list(src.shape), BF16).ap()
    w1_b, w2_b = bfcopy("w1", w1_d), bfcopy("w2", w2_d)
    abin_b, about_b = bfcopy("abin", abin_d), bfcopy("about", about_d)
    mlain_b, uq_b, ukv_b, mlaout_b = bfcopy("mlain", mlain_d), bfcopy("uq", uq_d), bfcopy("ukv", ukv_d), bfcopy("mlaout", mlaout_d)
    CV = {}

    def convert(name, src, dst, i):
        b = Buf("cv_%s%d" % (name, i))
        CV[(name, i)] = b
        R = src.shape[1]
        for j, r0 in enumerate(range(0, R, 128)):
            thr = []
            if j >= 4:
                tb = Buf("thr")
                tb.w = (b.dsem, b.dcnt - 48)
                thr = [tb]
            S.dma("pool", (lambda e, r0=r0: e.dma_start(out=dst[i, r0:r0 + 128, :], in_=src[i, r0:r0 + 128, :])),
                  reads=thr, writes=(b,), track=b)

    converted = set()

    def convert_layer(l):
        if l in converted or l not in layers:
            return
        converted.add(l)
        if STAGE >= 2:
            if l % 2 == 0:
                i = lay_even.index(l)
                convert("abin", abin_d, abin_b, i)
                convert("about", about_d, about_b, i)
            else:
                i = lay_odd.index(l)
                convert("mlain", mlain_d, mlain_b, i)
                convert("uq", uq_d, uq_b, i)
                convert("ukv", ukv_d, ukv_b, i)
                convert("mlaout", mlaout_d, mlaout_b, i)
        if l in lay_mlp:
            i = lay_mlp.index(l)
            convert("w1", w1_d, w1_b, i)
            convert("w2", w2_d, w2_b, i)

    def next_layer(l):
        idx = layers.index(l)
        return layers[idx + 1] if idx + 1 < len(layers) else None

    xT = sb("xT_sb", [P, NCH, TOK], F32)
    XB = [[Buf("x%d_%d" % (c, t)) for t in range(TOK // TM)] for c in range(NCH)]

    def xbufs(c, t0, n):
        return [XB[c][i] for i in range(t0 // TM, (t0 + n) // TM)]

    ARENA_N = 25600
    arena_t = sb("arena", [P, ARENA_N], BF16)
    AR = Arena(arena_t, ARENA_N)
    wsl_t = sb("wslots", [P, NSLOT, SLOT_ELEMS], BF16)
    WS = [Buf("ws%d" % i) for i in range(NSLOT)]
    ws_next = [0]
    ones_bf = sb("ones_bf", [P, P], BF16)
    tri_bf = sb("tri_bf", [P, P], BF16)
    modT = sb("modT", [P, 4 * 96], F32)
    a1 = sb("a1", [P, 64], F32)
    a2 = sb("a2", [P, 64], F32)
    n1g = sb("n1g_s", [P, 64], F32)
    n2g = sb("n2g_s", [P, 64], F32)
    smallc = sb("smallc", [P, 128], F32)
    CONST = Buf("const")
    MOD = Buf("mod")

    psum = [stack.enter_context(nc.psum_tensor("ps%d" % i, [P, 512], F32)) for i in range(8)]
    PSB = [Buf("psb%d" % i) for i in range(8)]
    ps_rr = [0]

    def next_ps(pool=(0, 1, 2, 3, 4, 5)):
        i = pool[ps_rr[0] % len(pool)]
        ps_rr[0] += 1
        return psum[i], PSB[i]

    extra_ws = []

    def next_ws():
        n = NSLOT + len(extra_ws)
        i = ws_next[0] % n
        ws_next[0] += 1
        if i < NSLOT:
            return wsl_t[:, i, :], WS[i]
        return extra_ws[i - NSLOT]

    def wload(dram_ap_fn, slot_ap, slot_buf, nsplit=2, q="pool", dep=()):
        for i in range(nsplit):
            dst, src = dram_ap_fn(i, nsplit)
            S.dma(q, (lambda e, dst=dst, src=src: e.dma_start(out=dst, in_=src)),
                  reads=dep, writes=(slot_buf,), track=slot_buf)

    def cload(dst, src):
        S.dma("sp", (lambda e, dst=dst, src=src: e.dma_start(out=dst, in_=src)), writes=(CONST,), track=CONST)

    cload(n1g[:], n1g_d[:, :])
    cload(n2g[:], n2g_d[:, :])
    cload(smallc[:, 0:1], rbias_d[:, :])
    cload(smallc[:, 1:2], cmask_d[:, :])
    cload(smallc[:, 2:4], qhgn_d[:, :])
    cload(smallc[:, 4:6], khgn_d[:, :])
    cload(smallc[:, 8:16], qng_d[:, :])
    cload(smallc[:, 16:24], kvng_d[:, :])
    cload(smallc[:, 24:40], convb_d[:, :])
    cload(smallc[:, 40:56], lng_d[:, :])
    cload(smallc[:, 56:72], lnb_d[:, :])
    cload(smallc[0:64, 72:76], qhgr_d[:, :])
    cload(smallc[0:64, 76:80], khgr_d[:, :])
    S.op("dve", lambda e: e.memset(ones_bf[:], 1.0), writes=(CONST,))
    S.op("dve", lambda e: e.memset(smallc[:, 127:128], EPS), writes=(CONST,))
    epsc = smallc[:, 127:128]
    rbias = smallc[:, 0:1]
    cmask = smallc[:, 1:2]

    xv = xT_d.rearrange("(c p) t -> p c t", p=P)
    XLOAD = Buf("xload")
    for c in range(NCH):
        S.dma("sp", (lambda e, c=c: e.dma_start(out=xT[:, c, :], in_=xv[:, c, :])),
              writes=[XB[c][t] for t in range(TOK // TM)], track=XLOAD)

    for c in range(NCH):
        for t in range(TOK // TM):
            XB[c][t].w = (XLOAD.dsem, XLOAD.dcnt)

    AR.reset()
    cT = AR.alloc([16, 4], F32)
    cact = AR.alloc([16, 4], BF16)
    modrow = AR.alloc([6144], F32)
    adab_s = AR.alloc([2, 1536], F32)
    ADB = [Buf("adab0"), Buf("adab1")]
    onehot = AR.alloc([1], F32)
    gath = AR.alloc([1, 1536], F32)
    PRO = Buf("pro")
    tri_f = AR.alloc([P], F32)
    TRIF = Buf("trif")
    S.dma("sp", lambda e: e.dma_start(out=tri_f, in_=tri_d[:, :]), writes=(TRIF,), track=TRIF)
    S.op("dve", lambda e: e.tensor_copy(out=tri_bf[:], in_=tri_f), reads=(TRIF,), writes=(CONST,))
    S.op("dve", lambda e: e.memset(smallc[:, 126:127], 0.0), writes=(CONST,))
    zeroc = smallc[:, 126:127]
    GATH = [Buf("gath0")]
    S.dma("sp", lambda e: e.dma_start(out=cT.rearrange("p a b -> p (a b)"), in_=cT_d[:, :]), writes=(PRO,), track=PRO)
    S.dma("sp", lambda e: e.dma_start(out=onehot[0:4, :], in_=onehot_d[:, :]), writes=(PRO,), track=PRO)
    CACT = Buf("cact")
    S.op("act", lambda e: e.activation(out=cact, in_=cT, func=AF.Silu), reads=(PRO,), writes=(CACT,))
    MODROW = Buf("modrow")
    for l in range(4):
        S.dma("sp", (lambda e, l=l: e.dma_start(out=adab_s[0:4, l % 2, :], in_=adab_d[l])), writes=(ADB[l % 2],), track=ADB[l % 2])
        for nt in range(3):
            pt, pb = next_ps()
            for half in range(2):
                sl, slb = next_ws()
                slv = sl.rearrange("p (k n) -> p k n", k=8)
                src = adaw_d[l].rearrange("(k p) n -> p k n", p=P)

                def mk(i, ns, slv=slv, src=src, half=half, nt=nt):
                    return (slv[:, 4 * i:4 * i + 4, :], src[:, half * 8 + 4 * i: half * 8 + 4 * i + 4, nt * 512:(nt + 1) * 512])
                wload(mk, sl, slb)

                def mm(e, slv=slv, half=half, pt=pt):
                    ins = None
                    for k in range(8):
                        kc = half * 8 + k
                        ins = e.matmul(pt[0:4, :], lhsT=cact[:, kc, :], rhs=slv[:, k, :], start=(kc == 0), stop=(kc == 15))
                    return ins
                S.op("pe", mm, reads=(CACT, slb), writes=(pb,))
            col = l * 1536 + nt * 512
            S.op("dve", (lambda e, pt=pt, col=col, l=l, nt=nt: e.tensor_tensor(out=modrow[0:4, col:col + 512], in0=pt[0:4, :],
                                                                   in1=adab_s[0:4, l % 2, nt * 512:(nt + 1) * 512], op=ALU.add)),
                 reads=(pb, ADB[l % 2]), writes=(MODROW,))
    MODIB = Buf("modib")
    MODOB = Buf("modob")
    S.dma("sp", lambda e: e.dma_start(out=modib[:, :], in_=modrow[0:4, :]), reads=(MODROW,), writes=(MODIB,), track=MODIB)
    S.dma("pool", lambda e: e.collective_compute("AllGather", ALU.bypass, replica_groups=[list(range(NCORES))],
                                                 ins=[modib.ap().opt()], outs=[modob.ap().opt()]),
          reads=(MODIB,), writes=(MODOB,), track=MODOB, inc=1)
    mps, mpb = next_ps()
    gi = 0
    modob_v = modob.ap().rearrange("(r b) (l n) -> b r l n", b=4, l=4)
    for l in range(4):
        for r in range(NCORES):
            g, gb = gath[:, 0, :], GATH[0]
            gi += 1
            S.dma("sp", (lambda e, g=g, r=r, l=l: e.dma_start(out=g[0:4, 0:1536], in_=modob_v[:, r, l, :])),
                  reads=(MODOB,), writes=(gb,), track=gb)

            def mm(e, g=g, r=r, l=l):
                ins = None
                for j in range(12):
                    colo = l * 96 + r * 12 + j
                    ins = e.matmul(mps[:, colo:colo + 1], lhsT=g[0:4, j * 128:(j + 1) * 128], rhs=onehot[0:4, 0:1],
                                   start=True, stop=True)
                return ins
            S.op("pe", mm, reads=(gb, PRO), writes=(mpb,))
    S.op("dve", lambda e: e.tensor_copy(out=modT[:], in_=mps[:, 0:384]), reads=(mpb,), writes=(MOD,))
    for l in range(4):
        S.op("dve", (lambda e, l=l: e.scalar_tensor_tensor(out=a1[:, l * 16:(l + 1) * 16], in0=modT[:, l * 96 + 16:l * 96 + 32],
                                                           scalar=1.0, in1=n1g[:, l * 16:(l + 1) * 16], op0=ALU.add, op1=ALU.mult)),
             reads=(MOD, CONST), writes=(MOD,))
        S.op("dve", (lambda e, l=l: e.scalar_tensor_tensor(out=a2[:, l * 16:(l + 1) * 16], in0=modT[:, l * 96 + 64:l * 96 + 80],
                                                           scalar=1.0, in1=n2g[:, l * 16:(l + 1) * 16], op0=ALU.add, op1=ALU.mult)),
             reads=(MOD, CONST), writes=(MOD,))
    phase_bufs = [PRO, CACT, MODROW, TRIF] + GATH + ADB
    if layers and STAGE >= 2:
        convert_layer(layers[0])

    def modcol(l, seg, c):
        return modT[:, l * 96 + seg * 16 + c: l * 96 + seg * 16 + c + 1]

    ALLENG = ("pe", "act", "dve", "sp")

    def new_phase(bufs):
        del extra_ws[:]
        S.barrier(bufs, ALLENG + ("pool",))
        S.retire(bufs)
        if max(E.cnt for E in S.E.values()) > 12000:
            S.fresh_progress()
        AR.reset()

    def rsqrt(out, in_, scale, rbufs, wbuf):
        npart = out.partition_size()
        S.op("act", (lambda e: e.activation(out=out, in_=in_, func=AF.Sqrt, bias=epsc[0:npart, :], scale=scale)),
             reads=list(rbufs) + [CONST], writes=(wbuf,))
        S.op("dve", (lambda e: e.reciprocal(out=out, in_=out)), reads=(wbuf,), writes=(wbuf,))

    def rms_modulate(l, which, t0, T, h, HB, sqs, SQB, rstd, RSB, tmp, TMPB):
        aa = a1 if which == 1 else a2
        seg_shift = 0 if which == 1 else 3
        pt, pb = next_ps()
        for c in range(NCH):
            q, qb = sqs[:, c % 2, 0:T], SQB[c % 2]
            S.op("act", (lambda e, c=c, q=q: e.activation(out=q, in_=xT[:, c, t0:t0 + T], func=AF.Square)),
                 reads=xbufs(c, t0, T), writes=(qb,))
            S.op("pe", (lambda e, c=c, q=q: e.matmul(pt[:, 0:T], lhsT=ones_bf[:], rhs=q, start=(c == 0), stop=(c == NCH - 1))),
                 reads=(qb, CONST), writes=(pb,))
        rsqrt(rstd[:, 0:T], pt[:, 0:T], 1.0 / D, (pb,), RSB)
        for c in range(NCH):
            tp, tb = tmp[:, c % 2, 0:T], TMPB[c % 2]
            S.op("dve", (lambda e, c=c, tp=tp: e.scalar_tensor_tensor(out=tp, in0=xT[:, c, t0:t0 + T], scalar=aa[:, l * 16 + c:l * 16 + c + 1],
                                                                      in1=rstd[:, 0:T], op0=ALU.mult, op1=ALU.mult)),
                 reads=xbufs(c, t0, T) + [RSB, MOD], writes=(tb,))
            S.op("act", (lambda e, c=c, tp=tp: e.activation(out=h[:, c, 0:T], in_=tp, func=AF.Identity, bias=modcol(l, seg_shift, c), scale=1.0)),
                 reads=(tb, MOD), writes=(HB[c],))

    def resid_update(l, seg_gate, oc, pt, pb, t0, T):
        S.op("dve", (lambda e: e.scalar_tensor_tensor(out=xT[:, oc, t0:t0 + T], in0=pt[:, 0:T], scalar=modcol(l, seg_gate, oc),
                                                      in1=xT[:, oc, t0:t0 + T], op0=ALU.mult, op1=ALU.add)),
             reads=[pb, MOD] + xbufs(oc, t0, T), writes=xbufs(oc, t0, T))

    def mlp_layer(l):
        nonlocal phase_bufs
        new_phase(phase_bufs)
        h = AR.alloc([NCH, TF], BF16)
        HB = [Buf("h%d" % c) for c in range(NCH)]
        sqs = AR.alloc([2, TF], BF16)
        SQB = [Buf("sq0"), Buf("sq1")]
        rstd = AR.alloc([TF], F32)
        RSB = Buf("rstd")
        tmp = AR.alloc([2, TF], F32)
        TMPB = [Buf("tmp0"), Buf("tmp1")]
        hid = AR.alloc([2, 4, TF], BF16)
        HIDB = [[Buf("hid%d_%d" % (i, j)) for j in range(4)] for i in range(2)]
        rf = AR.alloc([2, TF], F32)
        RFB = [Buf("rf0"), Buf("rf1")]
        phase_bufs = HB + SQB + [RSB] + TMPB + [b for r in HIDB for b in r] + RFB
        xs = AR.alloc([SLOT_ELEMS], BF16)
        XSB = Buf("xslot")
        extra_ws.append((xs, XSB))
        phase_bufs.append(XSB)
        w1v = w1_b[lay_mlp.index(l)].rearrange("(k p) n -> p k n", p=P)
        w2v = w2_b[lay_mlp.index(l)].rearrange("(k p) n -> p k n", p=P)
        for tt in range(TOK // TF):
            t0 = tt * TF
            rms_modulate(l, 2, t0, TF, h, HB, sqs, SQB, rstd, RSB, tmp, TMPB)
            for sbk in range(DFF // 512):
                hs = sbk % 2
                for wb in range(2):
                    sl, slb = next_ws()
                    slv = sl.rearrange("p (k n) -> p k n", k=16)
                    c0 = sbk * 512 + wb * 256

                    def mk(i, ns, slv=slv, c0=c0):
                        return (slv[:, 8 * i:8 * i + 8, :], w1v[:, 8 * i:8 * i + 8, c0:c0 + 256])
                    wload(mk, sl, slb, q="sp", dep=(CV[("w1", lay_mlp.index(l))],))
                    for j in range(2):
                        hc = wb * 2 + j
                        pt, pb = next_ps()

                        def mm(e, slv=slv, j=j, pt=pt):
                            ins = None
                            for kc in range(NCH):
                                ins = e.matmul(pt[:, 0:TF], lhsT=slv[:, kc, j * 128:(j + 1) * 128], rhs=h[:, kc, :],
                                               start=(kc == 0), stop=(kc == NCH - 1))
                            return ins
                        S.op("pe", mm, reads=HB + [slb], writes=(pb,))
                        r_, rb = rf[:, hc % 2, :], RFB[hc % 2]
                        S.op("act", (lambda e, pt=pt, r_=r_: e.activation(out=r_, in_=pt[:, 0:TF], func=AF.Relu)),
                             reads=(pb,), writes=(rb,))
                        S.op("dve", (lambda e, r_=r_, hs=hs, hc=hc: e.tensor_tensor(out=hid[:, hs, hc, :], in0=r_, in1=r_, op=ALU.mult)),
                             reads=(rb,), writes=(HIDB[hs][hc],))
                w2s = []
                for wb in range(2):
                    sl, slb = next_ws()
                    slv = sl.rearrange("p (k n) -> p k n", k=2)
                    k0 = sbk * 4 + wb * 2

                    def mk(i, ns, slv=slv, k0=k0):
                        return (slv[:, i:i + 1, :], w2v[:, k0 + i:k0 + i + 1, :])
                    wload(mk, sl, slb, q="sp", dep=(CV[("w2", lay_mlp.index(l))],))
                    w2s.append((slv, slb))
                for oc in range(NCH):
                    pt, pb = next_ps()

                    def mm(e, oc=oc, pt=pt, w2s=w2s, hs=hs):
                        ins = None
                        for hc in range(4):
                            slv = w2s[hc // 2][0]
                            ins = e.matmul(pt[:, 0:TF], lhsT=slv[:, hc % 2, oc * 128:(oc + 1) * 128], rhs=hid[:, hs, hc, :],
                                           start=(hc == 0), stop=(hc == 3))
                        return ins
                    S.op("pe", mm, reads=HIDB[hs] + [w2s[0][1], w2s[1][1]], writes=(pb,))
                    resid_update(l, 5, oc, pt, pb, t0, TF)

    def even_mixer(l):
        nonlocal phase_bufs
        e_ = l // 2
        new_phase(phase_bufs)
        h = AR.alloc([NCH, TM], BF16)
        HB = [Buf("h%d" % c) for c in range(NCH)]
        sqs = AR.alloc([2, TM], BF16)
        SQB = [Buf("sq0"), Buf("sq1")]
        rstd = AR.alloc([TM], F32)
        RSB = Buf("rstd")
        tmp = AR.alloc([2, TM], F32)
        TMPB = [Buf("tmp0"), Buf("tmp1")]
        cat = AR.alloc([8, TM], BF16)
        CATB = [Buf("cat%d" % c) for c in range(8)]
        gv = AR.alloc([2, 1024], F32)
        GVB = [Buf("gv0"), Buf("gv1")]
        vn = AR.alloc([2, 1024], BF16)
        VNB = [Buf("vn0"), Buf("vn1")]
        vst = AR.alloc([2, 16], F32)
        VSTB = [Buf("vst0"), Buf("vst1")]
        ybuf = AR.alloc([2, 30 + TM], F32)
        YB = [Buf("y0"), Buf("y1")]
        tails = AR.alloc([8, 32], F32)
        TAILB = [Buf("tail%d" % c) for c in range(8)]
        acc = AR.alloc([2, TM], F32)
        ACCB = [Buf("acc0"), Buf("acc1")]
        sg = AR.alloc([2, TM], F32)
        SGB = [Buf("sg0"), Buf("sg1")]
        lnst = AR.alloc([3, TM], F32)
        LNB = Buf("lnst")
        wm_bf = AR.alloc([8, P], BF16)
        wm_f = gv[:, 0, :].rearrange("p (a b) -> p a b", a=8)
        sgug = AR.alloc([1024], F32)
        sgub = AR.alloc([1024], F32)
        convw = AR.alloc([8, 31], F32)
        EC = Buf("evenconst")
        phase_bufs = HB + SQB + [RSB] + TMPB + CATB + GVB + VNB + VSTB + YB + TAILB + ACCB + SGB + [LNB, EC]
        S.dma("sp", lambda e: e.dma_start(out=wm_f.rearrange("p a b -> p (a b)"), in_=wmT_d[:, e_ * 1024:(e_ + 1) * 1024]), writes=(GVB[0],), track=GVB[0])
        S.dma("sp", lambda e: e.dma_start(out=sgug, in_=sgug_d[:, e_ * 1024:(e_ + 1) * 1024]), writes=(EC,), track=EC)
        S.dma("sp", lambda e: e.dma_start(out=sgub, in_=sgub_d[:, e_ * 1024:(e_ + 1) * 1024]), writes=(EC,), track=EC)
        S.dma("sp", lambda e: e.dma_start(out=convw.rearrange("p a b -> p (a b)"), in_=convw_d[:, e_ * 248:(e_ + 1) * 248]), writes=(EC,), track=EC)
        WMB = Buf("wm")
        phase_bufs.append(WMB)
        S.op("dve", lambda e: e.tensor_tensor(out=wm_bf, in0=wm_f, in1=tri_bf[:].unsqueeze(1).to_broadcast([P, 8, P]), op=ALU.mult),
             reads=(GVB[0], CONST), writes=(WMB,))
        convb = smallc[:, 24 + e_ * 8: 32 + e_ * 8]
        lng = smallc[:, 40 + e_ * 8: 48 + e_ * 8]
        lnb = smallc[:, 56 + e_ * 8: 64 + e_ * 8]
        winv = abin_b[lay_even.index(l)].rearrange("(k p) n -> p k n", p=P)
        woutv = about_b[lay_even.index(l)].rearrange("(k p) n -> p k n", p=P)

        def load_win(c0, ncols=256):
            sl, slb = next_ws()
            slv = sl.rearrange("p (k n) -> p k n", k=16)

            def mk(i, ns, slv=slv, c0=c0):
                return (slv[:, 8 * i:8 * i + 8, 0:ncols], winv[:, 8 * i:8 * i + 8, c0:c0 + ncols])
            wload(mk, sl, slb, q="sp", dep=(CV[("abin", lay_even.index(l))],))
            return slv, slb

        def proj_fm(slv, slb, j, T, pool=(0, 1, 2, 3, 4, 5)):
            pt, pb = next_ps(pool)

            def mm(e, slv=slv, j=j, pt=pt):
                ins = None
                for kc in range(NCH):
                    ins = e.matmul(pt[:, 0:T], lhsT=slv[:, kc, j * 128:(j + 1) * 128], rhs=h[:, kc, 0:T],
                                   start=(kc == 0), stop=(kc == NCH - 1))
                return ins
            S.op("pe", mm, reads=HB + [slb], writes=(pb,))
            return pt, pb

        GC1 = 1.0 / 0.044715
        GC2 = 2.0 * 0.7978845608028654 * 0.044715

        def gelu(src, srcb, T, out, outb, k):
            a, ab = acc[:, k % 2, 0:T], ACCB[k % 2]
            s_, s_b = sg[:, k % 2, 0:T], SGB[k % 2]
            S.op("act", (lambda e: e.activation(out=a, in_=src, func=AF.Square)), reads=(srcb,), writes=(ab,))
            S.op("dve", (lambda e: e.scalar_tensor_tensor(out=a, in0=a, scalar=GC1, in1=src, op0=ALU.add, op1=ALU.mult)), reads=(ab, srcb), writes=(ab,))
            S.op("act", (lambda e: e.activation(out=s_, in_=a, func=AF.Sigmoid, scale=GC2)), reads=(ab,), writes=(s_b,))
            S.op("dve", (lambda e: e.tensor_tensor(out=out, in0=src, in1=s_, op=ALU.mult)), reads=(srcb, s_b), writes=(outb,))

        def y_from_ag(c, T, ysl, ysb, pa, pab, pg, pgb, off):
            s_, s_b = sg[:, c % 2, 0:T], SGB[c % 2]
            S.op("act", (lambda e: e.activation(out=s_, in_=pg[:, 0:T], func=AF.Sigmoid)), reads=(pgb,), writes=(s_b,))
            S.op("dve", (lambda e: e.tensor_tensor(out=ysl[:, off:off + T], in0=pa[:, 0:T], in1=s_, op=ALU.mult)),
                 reads=(pab, s_b), writes=(ysb,))

        HIB, HOB = Buf("haloib"), Buf("haloob")
        t0 = TOK - 32
        rms_modulate(l, 1, t0, 32, h, HB, sqs, SQB, rstd, RSB, tmp, TMPB)
        for c in range(8):
            if c % 2 == 0:
                sa, sab = load_win(2048 + (c // 2) * 256)
                sgl, sglb = load_win(3072 + (c // 2) * 256)
            pa, pab = proj_fm(sa, sab, c % 2, 32)
            pg, pgb = proj_fm(sgl, sglb, c % 2, 32)
            ysl, ysb = ybuf[:, c % 2, :], YB[c % 2]
            y_from_ag(c, 32, ysl, ysb, pa, pab, pg, pgb, 0)
            S.dma("sp", (lambda e, c=c, ysl=ysl: e.dma_start(out=haloib[e_][c * 128:(c + 1) * 128, :], in_=ysl[:, 0:32])),
                  reads=(ysb,), writes=(HIB,), track=HIB)
        S.dma("pool", lambda e: e.collective_compute("AllGather", ALU.bypass, replica_groups=PAIRS,
                                                     ins=[haloib[e_].ap().opt()], outs=[haloob[e_].ap().opt()]),
              reads=(HIB,), writes=(HOB,), track=HOB, inc=1)
        if next_layer(l) is not None:
            convert_layer(next_layer(l))
        for c in range(8):
            S.dma("sp", (lambda e, c=c: e.dma_start(out=tails[:, c, :], in_=haloob[e_][c * 128:(c + 1) * 128, :])),
                  reads=(HOB,), writes=(TAILB[c],), track=TAILB[c])
            S.op("dve", (lambda e, c=c: e.tensor_scalar(out=tails[:, c, :], in0=tails[:, c, :], scalar1=cmask, scalar2=None, op0=ALU.mult)),
                 reads=(TAILB[c], CONST), writes=(TAILB[c],))

        for tt in range(TOK // TM):
            if STAGE == 2:
                break
            t0 = tt * TM
            rms_modulate(l, 1, t0, TM, h, HB, sqs, SQB, rstd, RSB, tmp, TMPB)
            for j4 in range(4):
                slv, slb = load_win(j4 * 256)
                for j in range(2):
                    g = j4 * 2 + j
                    pt, pb = proj_fm(slv, slb, j, TM)
                    gelu(pt[:, 0:TM], pb, TM, cat[:, g, :], CATB[g], g)
            if SUB == 1:
                break
            for j4 in range(4):
                slv, slb = load_win(1024 + j4 * 256)
                for tc in range(2):
                    pt, pb = next_ps()

                    def mm(e, slv=slv, tc=tc, pt=pt):
                        ins = None
                        for kc in range(NCH):
                            ins = e.matmul(pt[:, 0:256], lhsT=h[:, kc, tc * 128:(tc + 1) * 128], rhs=slv[:, kc, :],
                                           start=(kc == 0), stop=(kc == NCH - 1))
                        return ins
                    S.op("pe", mm, reads=HB + [slb], writes=(pb,))
                    gelu(pt[:, 0:256], pb, 256, gv[:, tc, j4 * 256:(j4 + 1) * 256], GVB[tc], tc)
            if SUB == 11:
                break
            for tc in range(2):
                S.op("act", (lambda e, tc=tc: e.activation(out=vn[:, tc, :], in_=gv[:, tc, :], func=AF.Square)),
                     reads=(GVB[tc],), writes=(VNB[tc],))
                S.op("dve", (lambda e, tc=tc: e.tensor_reduce(out=vst[:, tc, 0:8], in_=vn[:, tc, :].rearrange("p (g d) -> p g d", g=8),
                                                              axis=mybir.AxisListType.X, op=ALU.add)),
                     reads=(VNB[tc],), writes=(VSTB[tc],))
                rsqrt(vst[:, tc, 8:16], vst[:, tc, 0:8], 1.0 / 128, (VSTB[tc],), VSTB[tc])
                if SUB == 12:
                    continue
                for g in range(8):
                    S.op("dve", (lambda e, tc=tc, g=g: e.scalar_tensor_tensor(out=vn[:, tc, g * 128:(g + 1) * 128], in0=gv[:, tc, g * 128:(g + 1) * 128],
                                                                              scalar=vst[:, tc, 8 + g:9 + g], in1=sgug[:, g * 128:(g + 1) * 128],
                                                                              op0=ALU.mult, op1=ALU.mult)),
                         reads=(GVB[tc], VSTB[tc], EC), writes=(VNB[tc],))
                if SUB == 13:
                    continue
                for half in range(2):
                    pt, pb = next_ps()

                    def mm(e, tc=tc, half=half, pt=pt):
                        ins = None
                        for gg in range(4):
                            g = half * 4 + gg
                            ins = e.matmul(pt[:, gg * 128:(gg + 1) * 128], lhsT=vn[:, tc, g * 128:(g + 1) * 128], rhs=wm_bf[:, g, :],
                                           start=True, stop=True)
                        return ins
                    S.op("pe", mm, reads=(VNB[tc], WMB), writes=(pb,))
                    mixt, mixb = (acc if half == 0 else sg), (ACCB if half == 0 else SGB)
                    mv = mixt.rearrange("p a b -> p (a b)")
                    S.op("dve", (lambda e, pt=pt, half=half, mv=mv: e.tensor_tensor(out=mv, in0=pt[:, 0:512], in1=sgub[:, half * 512:(half + 1) * 512], op=ALU.add)),
                         reads=(pb, EC), writes=mixb)
                    for gg in range(4):
                        g = half * 4 + gg
                        S.op("dve", (lambda e, g=g, gg=gg, tc=tc, mv=mv: e.tensor_tensor(out=cat[:, g, tc * 128:(tc + 1) * 128], in0=mv[:, gg * 128:(gg + 1) * 128],
                                                                                     in1=cat[:, g, tc * 128:(tc + 1) * 128], op=ALU.mult)),
                             reads=list(mixb) + [CATB[g]], writes=(CATB[g],))
            if SUB in (2, 12, 13):
                break
            out_proj_half(l, woutv, 0, cat, CATB, t0)
            if SUB == 3:
                break
            s1p, s1b = psum[6], PSB[6]
            s2p, s2b = psum[7], PSB[7]
            for c in range(8):
                if c % 2 == 0:
                    sa, sab = load_win(2048 + (c // 2) * 256)
                    sgl, sglb = load_win(3072 + (c // 2) * 256)
                pa, pab = proj_fm(sa, sab, c % 2, TM)
                pg, pgb = proj_fm(sgl, sglb, c % 2, TM)
                ysl, ysb = ybuf[:, c % 2, :], YB[c % 2]
                S.op("act", (lambda e, c=c, ysl=ysl: e.activation(out=ysl[:, 0:30], in_=tails[:, c, 2:32], func=AF.Copy)),
                     reads=(TAILB[c],), writes=(ysb,))
                y_from_ag(c, TM, ysl, ysb, pa, pab, pg, pgb, 30)
                S.op("act", (lambda e, c=c, ysl=ysl: e.activation(out=tails[:, c, 2:32], in_=ysl[:, TM:TM + 30], func=AF.Copy)),
                     reads=(ysb,), writes=(TAILB[c],))
                ac, acb = acc[:, c % 2, :], ACCB[c % 2]

                def conv(e, c=c, ysl=ysl, ac=ac):
                    ins = e.tensor_scalar(out=ac, in0=ysl[:, 0:TM], scalar1=convw[:, c, 0:1], scalar2=convb[:, c:c + 1], op0=ALU.mult, op1=ALU.add)
                    return ins
                S.op("dve", conv, reads=(ysb, EC, CONST), writes=(acb,))
                for j in range(1, 31):
                    S.op("dve", (lambda e, c=c, ysl=ysl, ac=ac, j=j: e.scalar_tensor_tensor(out=ac, in0=ysl[:, j:j + TM], scalar=convw[:, c, j:j + 1],
                                                                                         in1=ac, op0=ALU.mult, op1=ALU.add)),
                         reads=(ysb, EC, acb), writes=(acb,))
                S.op("act", (lambda e, c=c, ac=ac: e.activation(out=cat[:, c, :], in_=ac, func=AF.Copy)), reads=(acb,), writes=(CATB[c],))
                q, qb = sqs[:, c % 2, :], SQB[c % 2]
                S.op("act", (lambda e, q=q, ac=ac: e.activation(out=q, in_=ac, func=AF.Square)), reads=(acb,), writes=(qb,))
                S.op("pe", (lambda e, c=c: e.matmul(s1p[:, 0:TM], lhsT=ones_bf[:], rhs=cat[:, c, :], start=(c == 0), stop=(c == 7))),
                     reads=(CATB[c], CONST), writes=(s1b,))
                S.op("pe", (lambda e, c=c, q=q: e.matmul(s2p[:, 0:TM], lhsT=ones_bf[:], rhs=q, start=(c == 0), stop=(c == 7))),
                     reads=(qb, CONST), writes=(s2b,))
            if SUB == 4:
                break
            mean, rs, nmr = lnst[:, 0, :], lnst[:, 1, :], lnst[:, 2, :]
            S.op("dve", lambda e: e.tensor_scalar(out=mean, in0=s1p[:, 0:TM], scalar1=1.0 / 1024, scalar2=None, op0=ALU.mult), reads=(s1b,), writes=(LNB,))
            S.op("dve", lambda e: e.tensor_tensor(out=nmr, in0=mean, in1=mean, op=ALU.mult), reads=(LNB,), writes=(LNB,))
            S.op("dve", lambda e: e.scalar_tensor_tensor(out=rs, in0=s2p[:, 0:TM], scalar=1.0 / 1024, in1=nmr, op0=ALU.mult, op1=ALU.subtract),
                 reads=(s2b, LNB), writes=(LNB,))
            rsqrt(rs, rs, 1.0, (LNB,), LNB)
            S.op("dve", lambda e: e.scalar_tensor_tensor(out=nmr, in0=mean, scalar=-1.0, in1=rs, op0=ALU.mult, op1=ALU.mult), reads=(LNB,), writes=(LNB,))
            for c in range(8):
                tp, tb = tmp[:, c % 2, :], TMPB[c % 2]
                S.op("dve", (lambda e, c=c, tp=tp: e.tensor_tensor(out=tp, in0=rs, in1=cat[:, c, :], op=ALU.mult)), reads=(CATB[c], LNB), writes=(tb,))
                S.op("dve", (lambda e, c=c, tp=tp: e.tensor_tensor(out=tp, in0=tp, in1=nmr, op=ALU.add)), reads=(tb, LNB), writes=(tb,))
                S.op("act", (lambda e, c=c, tp=tp: e.activation(out=cat[:, c, :], in_=tp, func=AF.Silu, bias=lnb[:, c:c + 1], scale=lng[:, c:c + 1])),
                     reads=(tb, CONST), writes=(CATB[c],))
            out_proj_half(l, woutv, 1, cat, CATB, t0)

    def out_proj_half(l, woutv, part, cat, CATB, t0):
        for q4 in range(4):
            sl, slb = next_ws()
            slv = sl.rearrange("p (k n) -> p k n", k=8)

            def mk(i, ns, slv=slv, q4=q4):
                return (slv[:, 4 * i:4 * i + 4, :], woutv[:, part * 8 + 4 * i: part * 8 + 4 * i + 4, q4 * 512:(q4 + 1) * 512])
            wload(mk, sl, slb, q="sp", dep=(CV[("about", lay_even.index(l))],))
            for j in range(4):
                oc = q4 * 4 + j
                pt, pb = next_ps()

                def mm(e, slv=slv, j=j, pt=pt):
                    ins = None
                    for kc in range(8):
                        ins = e.matmul(pt[:, 0:TM], lhsT=slv[:, kc, j * 128:(j + 1) * 128], rhs=cat[:, kc, :], start=(kc == 0), stop=(kc == 7))
                    return ins
                S.op("pe", mm, reads=CATB + [slb], writes=(pb,))
                resid_update(l, 2, oc, pt, pb, t0, TM)


    def proj_chunk(slv, slb, h, HB, col0, ncols, T, nk=NCH, pool=(0, 1, 2, 3, 4, 5)):
        pt, pb = next_ps(pool)

        def mm(e):
            ins = None
            for kc in range(nk):
                ins = e.matmul(pt[0:ncols, 0:T], lhsT=slv[:, kc, col0:col0 + ncols], rhs=h[:, kc, 0:T], start=(kc == 0), stop=(kc == nk - 1))
            return ins
        S.op("pe", mm, reads=list(HB) + [slb], writes=(pb,))
        return pt, pb

    def sumsq(src, srcb, nrow, T, sqs, SQB, k):
        q, qb = sqs[0:nrow, k % 2, 0:T], SQB[k % 2]
        S.op("act", (lambda e: e.activation(out=q, in_=src, func=AF.Square)), reads=(srcb,), writes=(qb,))
        pt, pb = next_ps()
        S.op("pe", (lambda e: e.matmul(pt[0:nrow, 0:T], lhsT=ones_bf[0:nrow, 0:nrow], rhs=q, start=True, stop=True)), reads=(qb, CONST), writes=(pb,))
        return pt, pb

    def rope_combine(pa, pab, pb2, pbb, g0, g1, rs2, RS2B, cs, sn, TABB, ta, TAB, tb2, TBB, out, outb, T):
        S.op("dve", (lambda e: e.scalar_tensor_tensor(out=ta[0:64, 0:T], in0=pa[0:64, 0:T], scalar=g0, in1=rs2[0:64, 0:T], op0=ALU.mult, op1=ALU.mult)),
             reads=(pab, RS2B, CONST), writes=(TAB,))
        S.op("dve", (lambda e: e.tensor_tensor(out=ta[0:64, 0:T], in0=ta[0:64, 0:T], in1=cs[0:64, 0:T], op=ALU.mult)), reads=(TAB, TABB), writes=(TAB,))
        S.op("dve", (lambda e: e.scalar_tensor_tensor(out=tb2[0:64, 0:T], in0=pb2[0:64, 0:T], scalar=g1, in1=rs2[0:64, 0:T], op0=ALU.mult, op1=ALU.mult)),
             reads=(pbb, RS2B, CONST), writes=(TBB,))
        S.op("dve", (lambda e: e.tensor_tensor(out=tb2[0:64, 0:T], in0=tb2[0:64, 0:T], in1=sn[0:64, 0:T], op=ALU.mult)), reads=(TBB, TABB), writes=(TBB,))
        S.op("dve", (lambda e: e.tensor_tensor(out=out, in0=ta[0:64, 0:T], in1=tb2[0:64, 0:T], op=ALU.add)), reads=(TAB, TBB), writes=(outb,))

    def mla_mixer(l):
        nonlocal phase_bufs
        o = l // 2
        oi = lay_odd.index(l)
        qng = smallc[:, 8 + o * 4: 12 + o * 4]
        kvng = smallc[:, 16 + o * 4: 20 + o * 4]
        qhgn = smallc[:, 2 + o: 3 + o]
        khgn = smallc[:, 4 + o: 5 + o]
        qhgr = smallc[0:64, 72 + 2 * o: 74 + 2 * o]
        khgr = smallc[0:64, 76 + 2 * o: 78 + 2 * o]
        new_phase(phase_bufs)
        h = AR.alloc([NCH, TM], BF16)
        HB = [Buf("h%d" % c) for c in range(NCH)]
        sqs = AR.alloc([2, TM], BF16)
        SQB = [Buf("sq0"), Buf("sq1")]
        rstd = AR.alloc([TM], F32)
        RSB = Buf("rstd")
        tmp = AR.alloc([2, TM], F32)
        TMPB = [Buf("tmp0"), Buf("tmp1")]
        cf = AR.alloc([4, TM], F32)
        CFB = [Buf("cf%d" % c) for c in range(4)]
        cn = AR.alloc([2, 4, TM], BF16)
        CNB = [Buf("cn0"), Buf("cn1")]
        kr = AR.alloc([TM], BF16)
        KRB = Buf("kr")
        cs = AR.alloc([TM], F32)
        sn = AR.alloc([TM], F32)
        TABB = Buf("tab")
        ta = AR.alloc([TM], F32)
        TAB = Buf("ta")
        tb2 = AR.alloc([TM], F32)
        TBB = Buf("tb")
        rs2 = AR.alloc([TM], F32)
        RS2B = Buf("rs2")
        phase_bufs = HB + SQB + [RSB] + TMPB + CFB + CNB + [KRB, TABB, TAB, TBB, RS2B]
        winv = mlain_b[oi].rearrange("(k p) n -> p k n", p=P)
        LATL = Buf("latloc%d" % o)
        CQS = Buf("cqscr%d" % o)
        KVOB = [Buf("kvob%d_%d" % (o, t)) for t in range(8)]

        def load_win(c0, ncols):
            sl, slb = next_ws()
            slv = sl.rearrange("p (k n) -> p k n", k=16)

            def mk(i, ns):
                return (slv[:, 8 * i:8 * i + 8, 0:ncols], winv[:, 8 * i:8 * i + 8, c0:c0 + ncols])
            wload(mk, sl, slb, q="sp", dep=(CV[("mlain", oi)],))
            return slv, slb

        for tt in range(TOK // TM):
            t0 = tt * TM
            KVIB = Buf("kvib%d_%d" % (o, tt))
            rms_modulate(l, 1, t0, TM, h, HB, sqs, SQB, rstd, RSB, tmp, TMPB)
            S.dma("sp", (lambda e, t0=t0: e.dma_start(out=cs[0:64, :], in_=cos2_d[:, t0:t0 + TM])), writes=(TABB,), track=TABB)
            S.dma("sp", (lambda e, t0=t0: e.dma_start(out=sn[0:64, :], in_=sin2_d[:, t0:t0 + TM])), writes=(TABB,), track=TABB)
            for which in range(2):
                ssp, ssb = psum[6], PSB[6]
                for half in range(2):
                    slv, slb = load_win(which * 512 + half * 256, 256)
                    for j in range(2):
                        c = half * 2 + j
                        pt, pb = proj_chunk(slv, slb, h, HB, j * 128, 128, TM)
                        S.op("act", (lambda e, pt=pt, c=c: e.activation(out=cf[:, c, :], in_=pt[:, 0:TM], func=AF.Copy)), reads=(pb,), writes=(CFB[c],))
                        q, qb = sqs[:, c % 2, :], SQB[c % 2]
                        S.op("act", (lambda e, q=q, c=c: e.activation(out=q, in_=cf[:, c, :], func=AF.Square)), reads=(CFB[c],), writes=(qb,))
                        S.op("pe", (lambda e, q=q, c=c: e.matmul(ssp[:, 0:TM], lhsT=ones_bf[:], rhs=q, start=(c == 0), stop=(c == 3))),
                             reads=(qb, CONST), writes=(ssb,))
                rsqrt(rs2[:, :], ssp[:, 0:TM], 1.0 / 512, (ssb,), RS2B)
                gvec = qng if which == 0 else kvng
                for c in range(4):
                    S.op("dve", (lambda e, c=c, which=which, gvec=gvec: e.scalar_tensor_tensor(out=cn[:, which, c, :], in0=cf[:, c, :], scalar=gvec[:, c:c + 1],
                                                                                          in1=rs2[:, :], op0=ALU.mult, op1=ALU.mult)),
                         reads=(CFB[c], RS2B, CONST), writes=(CNB[which],))
                if which == 0:
                    S.dma("sp", (lambda e, t0=t0: e.dma_start(out=cqscr[o][:, t0:t0 + TM].rearrange("(c p) t -> p c t", p=P), in_=cn[:, 0, :, :])),
                          reads=(CNB[0],), writes=(CQS,), track=CQS)
                else:
                    S.dma("sp", (lambda e, tt=tt: e.dma_start(out=kvib[(o, tt)][0:512, :].rearrange("(c p) t -> p c t", p=P), in_=cn[:, 1, :, :])),
                          reads=(CNB[1],), writes=(KVIB,), track=KVIB)
                    S.dma("sp", (lambda e, t0=t0: e.dma_start(out=latloc[o][0:512, t0:t0 + TM].rearrange("(c p) t -> p c t", p=P), in_=cn[:, 1, :, :])),
                          reads=(CNB[1],), writes=(LATL,), track=LATL)
            slv, slb = load_win(1024, 128)
            pa, pab = proj_chunk(slv, slb, h, HB, 0, 64, TM)
            pb2, pbb = proj_chunk(slv, slb, h, HB, 64, 64, TM)
            sp_, spb = sumsq(pa[0:64, 0:TM], pab, 64, TM, sqs, SQB, 0)
            rsqrt(rs2[0:64, :], sp_[0:64, 0:TM], 1.0 / 64, (spb,), RS2B)
            rope_combine(pa, pab, pb2, pbb, khgr[:, 0:1], khgr[:, 1:2], rs2, RS2B, cs, sn, TABB, ta, TAB, tb2, TBB, kr[0:64, :], KRB, TM)
            S.dma("sp", (lambda e, tt=tt: e.dma_start(out=kvib[(o, tt)][512:576, :], in_=kr[0:64, :])), reads=(KRB,), writes=(KVIB,), track=KVIB)
            S.dma("sp", (lambda e, t0=t0: e.dma_start(out=latloc[o][512:576, t0:t0 + TM], in_=kr[0:64, :])), reads=(KRB,), writes=(LATL,), track=LATL)
            S.dma("pool", (lambda e, tt=tt: e.collective_compute("AllGather", ALU.bypass, replica_groups=PAIRS,
                                                                 ins=[kvib[(o, tt)].ap().opt()], outs=[kvob[(o, tt)].ap().opt()])),
                  reads=(KVIB,), writes=(KVOB[tt],), track=KVOB[tt], inc=1)
        if next_layer(l) is not None:
            convert_layer(next_layer(l))
        if STAGE == 5:
            return
        mla_attn(l, LATL, CQS, KVOB)

    def mla_attn(l, LATL, CQS, KVOB):
        nonlocal phase_bufs
        o = l // 2
        oi = lay_odd.index(l)
        qhgn = smallc[:, 2 + o: 3 + o]
        khgn = smallc[:, 4 + o: 5 + o]
        qhgr = smallc[0:64, 72 + 2 * o: 74 + 2 * o]
        new_phase(phase_bufs)
        KT = AR.alloc([SEQ], BF16)
        KTB = [Buf("kt%d" % i) for i in range(16)]
        VT = AR.alloc([32, P], BF16)
        VTB = [Buf("vt%d" % i) for i in range(16)]
        KR = AR.alloc([SEQ], BF16)
        KRALL = Buf("krall")
        KRB2 = [KRALL for i in range(16)]
        lat = AR.alloc([4, TM], BF16)
        LATB = Buf("lat")
        cqt = AR.alloc([4, 512], BF16)
        CQTB = Buf("cqt")
        Qn = AR.alloc([2, 512], BF16)
        QNB = [Buf("qn0"), Buf("qn1")]
        Qr = AR.alloc([2, 512], BF16)
        QRB = [Buf("qr0"), Buf("qr1")]
        Et = AR.alloc([2, 512], BF16)
        ETB = [Buf("et0"), Buf("et1")]
        attn = AR.alloc([512], BF16)
        ATB = Buf("attn")
        ta = AR.alloc([512], F32)
        TAB = Buf("ta")
        tb2 = AR.alloc([512], F32)
        TBB = Buf("tb")
        cs = AR.alloc([512], F32)
        sn = AR.alloc([512], F32)
        TABB = Buf("tab")
        rs2 = AR.alloc([512], F32)
        RS2B = Buf("rs2")
        sqs = AR.alloc([2, 512], BF16)
        SQB = [Buf("sq0"), Buf("sq1")]
        trineg = AR.alloc([P], F32)
        TRNB = Buf("trineg")
        sdg = AR.alloc([P], F32)
        SDGB = Buf("sdg")
        S.dma("sp", lambda e: e.dma_start(out=trineg, in_=trineg_d[:, :]), writes=(TRNB,), track=TRNB)
        phase_bufs = KTB + VTB + [KRALL, LATB, CQTB, TRNB, SDGB] + QNB + QRB + ETB + [ATB, TAB, TBB, TABB, RS2B] + SQB
        uqv = uq_b[oi].rearrange("(k p) n -> p k n", p=P)
        ukvv = ukv_b[oi].rearrange("(k p) n -> p k n", p=P)
        Op, Ob = psum[6], PSB[6]
        Lp, Lb = psum[7], PSB[7]
        for hd in range(16):
            sl, slb = next_ws()
            wuq = sl[:, 0:1024].rearrange("p (k n) -> p k n", k=4)
            wukv = sl[:, 1024:2048].rearrange("p (k n) -> p k n", k=4)
            S.dma("sp", (lambda e, wuq=wuq, hd=hd: e.dma_start(out=wuq, in_=uqv[:, :, hd * 256:(hd + 1) * 256])), reads=(CV[("uq", oi)],), writes=(slb,), track=slb)
            S.dma("sp", (lambda e, wukv=wukv, hd=hd: e.dma_start(out=wukv, in_=ukvv[:, :, hd * 256:(hd + 1) * 256])), reads=(CV[("ukv", oi)],), writes=(slb,), track=slb)
            sl2, sl2b = next_ws()
            wout = sl2[:, 0:2048]
            S.dma("sp", (lambda e, wout=wout, hd=hd: e.dma_start(out=wout, in_=mlaout_b[oi][hd * 128:(hd + 1) * 128, :])), reads=(CV[("mlaout", oi)],), writes=(sl2b,), track=sl2b)
            for kt in range(16):
                if kt < 8:
                    src_lat = kvob[(o, kt)][0:512, :]
                    src_kr = kvob[(o, kt)][512:576, :]
                    srcb = KVOB[kt]
                else:
                    src_lat = latloc[o][0:512, (kt - 8) * TM:(kt - 7) * TM]
                    src_kr = latloc[o][512:576, (kt - 8) * TM:(kt - 7) * TM]
                    srcb = LATL
                S.dma("sp", (lambda e, src_lat=src_lat: e.dma_start(out=lat, in_=src_lat.rearrange("(c p) t -> p c t", p=P))),
                      reads=(srcb,), writes=(LATB,), track=LATB)
                if hd == 0:
                    S.dma("sp", (lambda e, src_kr=src_kr, kt=kt: e.dma_start(out=KR[0:64, kt * TM:(kt + 1) * TM], in_=src_kr)),
                          reads=(srcb,), writes=(KRB2[kt],), track=KRB2[kt])
                pk, pkb = proj_chunk(wukv, slb, lat, [LATB], 0, 128, TM, nk=4)
                sp_, spb = sumsq(pk[:, 0:TM], pkb, 128, TM, sqs, SQB, kt)
                rsqrt(rs2[:, 0:TM], sp_[:, 0:TM], 1.0 / 128, (spb,), RS2B)
                S.op("dve", (lambda e, pk=pk, kt=kt: e.scalar_tensor_tensor(out=KT[:, kt * TM:(kt + 1) * TM], in0=pk[:, 0:TM], scalar=khgn, in1=rs2[:, 0:TM],
                                                                         op0=ALU.mult, op1=ALU.mult)), reads=(pkb, RS2B, CONST), writes=(KTB[kt],))
                for tc in range(2):
                    pv, pvb = next_ps()

                    def mmv(e, pv=pv, tc=tc, wukv=wukv):
                        ins = None
                        for kc in range(4):
                            ins = e.matmul(pv[:, 0:128], lhsT=lat[:, kc, tc * 128:(tc + 1) * 128], rhs=wukv[:, kc, 128:256], start=(kc == 0), stop=(kc == 3))
                        return ins
                    S.op("pe", mmv, reads=(LATB, slb), writes=(pvb,))
                    S.op("act", (lambda e, pv=pv, kt=kt, tc=tc: e.activation(out=VT[:, kt * 2 + tc, :], in_=pv[:, 0:128], func=AF.Copy)),
                         reads=(pvb,), writes=(VTB[kt],))
            for jt in range(4):
                q0 = jt * 512
                qs = jt % 2
                S.dma("sp", (lambda e, q0=q0: e.dma_start(out=cqt, in_=cqscr[o][:, q0:q0 + 512].rearrange("(c p) t -> p c t", p=P))),
                      reads=(CQS,), writes=(CQTB,), track=CQTB)
                S.dma("sp", (lambda e, q0=q0: e.dma_start(out=cs[0:64, :], in_=cos2_d[:, q0:q0 + 512])), writes=(TABB,), track=TABB)
                S.dma("sp", (lambda e, q0=q0: e.dma_start(out=sn[0:64, :], in_=sin2_d[:, q0:q0 + 512])), writes=(TABB,), track=TABB)
                pq, pqb = proj_chunk(wuq, slb, cqt, [CQTB], 0, 128, 512, nk=4)
                sp_, spb = sumsq(pq[:, 0:512], pqb, 128, 512, sqs, SQB, 0)
                rsqrt(rs2[:, :], sp_[:, 0:512], 1.0 / 128, (spb,), RS2B)
                S.op("dve", (lambda e, pq=pq, qs=qs: e.scalar_tensor_tensor(out=Qn[:, qs, :], in0=pq[:, 0:512], scalar=qhgn, in1=rs2[:, :], op0=ALU.mult, op1=ALU.mult)),
                     reads=(pqb, RS2B, CONST), writes=(QNB[qs],))
                pa, pab = proj_chunk(wuq, slb, cqt, [CQTB], 128, 64, 512, nk=4)
                pb2, pbb = proj_chunk(wuq, slb, cqt, [CQTB], 192, 64, 512, nk=4)
                sp_, spb = sumsq(pa[0:64, 0:512], pab, 64, 512, sqs, SQB, 1)
                rsqrt(rs2[0:64, :], sp_[0:64, 0:512], 1.0 / 64, (spb,), RS2B)
                rope_combine(pa, pab, pb2, pbb, qhgr[:, 0:1], qhgr[:, 1:2], rs2, RS2B, cs, sn, TABB, ta, TAB, tb2, TBB, Qr[0:64, qs, :], QRB[qs], 512)
                blocks = [(kb, True) for kb in range(16)] + [(kb, False) for kb in range(4 * jt + 4)]
                nb = len(blocks)
                pend = None
                for idx, (kb, remote) in enumerate(blocks):
                    kcol = kb * 128 if remote else 2048 + kb * 128
                    kti = kcol // TM
                    vch = kcol // 128
                    diag = (not remote) and kb >= 4 * jt
                    qa = (kb - 4 * jt) * 128 if diag else 0
                    ps_, psb_ = next_ps()

                    def mms(e, ps_=ps_, kcol=kcol, qa=qa, qs=qs):
                        e.matmul(ps_[:, qa:512], lhsT=KT[:, kcol:kcol + 128], rhs=Qn[:, qs, qa:512], start=True, stop=False)
                        return e.matmul(ps_[:, qa:512], lhsT=KR[0:64, kcol:kcol + 128], rhs=Qr[0:64, qs, qa:512], start=False, stop=True)
                    S.op("pe", mms, reads=(KTB[kti], KRB2[kti], QNB[qs], QRB[qs]), writes=(psb_,))
                    es = idx % 2
                    bias = rbias if remote else zeroc
                    if diag:
                        S.op("dve", (lambda e, ps_=ps_, qa=qa: e.tensor_tensor(out=sdg[:, :], in0=ps_[:, qa:qa + 128], in1=trineg[:, :], op=ALU.add)),
                             reads=(psb_, TRNB), writes=(SDGB,))
                        S.op("act", (lambda e, qa=qa, es=es: e.activation(out=Et[:, es, qa:qa + 128], in_=sdg[:, :], func=AF.Exp, bias=zeroc, scale=ATTN_SCALE)),
                             reads=(SDGB, CONST), writes=(ETB[es],))
                        if qa + 128 < 512:
                            S.op("act", (lambda e, ps_=ps_, qa=qa, es=es: e.activation(out=Et[:, es, qa + 128:512], in_=ps_[:, qa + 128:512], func=AF.Exp, bias=zeroc, scale=ATTN_SCALE)),
                                 reads=(psb_, CONST), writes=(ETB[es],))
                    else:
                        S.op("act", (lambda e, ps_=ps_, qa=qa, es=es, bias=bias: e.activation(out=Et[:, es, qa:512], in_=ps_[:, qa:512], func=AF.Exp, bias=bias, scale=ATTN_SCALE)),
                             reads=(psb_, CONST), writes=(ETB[es],))

                    def mmo(e, qa=qa, es=es, vch=vch, idx=idx, nb=nb):
                        e.matmul(Op[:, qa:512], lhsT=VT[:, vch, :], rhs=Et[:, es, qa:512], start=(idx == 0), stop=(idx == nb - 1))
                        return e.matmul(Lp[:, qa:512], lhsT=ones_bf[:], rhs=Et[:, es, qa:512], start=(idx == 0), stop=(idx == nb - 1))
                    if pend is not None:
                        S.op("pe", pend[0], reads=pend[1], writes=(Ob, Lb))
                    pend = (mmo, (VTB[kti], ETB[es], CONST))
                S.op("pe", pend[0], reads=pend[1], writes=(Ob, Lb))
                S.op("dve", (lambda e: e.reciprocal(out=ta[:, :], in_=Lp[:, 0:512])), reads=(Lb,), writes=(TAB,))
                S.op("dve", (lambda e: e.tensor_tensor(out=attn[:, :], in0=Op[:, 0:512], in1=ta[:, :], op=ALU.mult)), reads=(Ob, TAB), writes=(ATB,))
                for oc in range(NCH):
                    pt, pb = next_ps()
                    S.op("pe", (lambda e, pt=pt, oc=oc, wout=wout: e.matmul(pt[:, 0:512], lhsT=wout[:, oc * 128:(oc + 1) * 128], rhs=attn[:, :], start=True, stop=True)),
                         reads=(ATB, sl2b), writes=(pb,))
                    resid_update(l, 2, oc, pt, pb, q0, 512)

    for l in layers:
        if STAGE < 2:
            break
        if l % 2 == 0:
            even_mixer(l)
        else:
            mla_mixer(l)
        if STAGE >= 4 and STAGE not in (5, 6):
            mlp_layer(l)

    ov = outT_d.rearrange("(c p) t -> p c t", p=P)
    OUTB = Buf("out")
    for c in range(NCH):
        S.dma("sp", (lambda e, c=c: e.dma_start(out=ov[:, c, :], in_=xT[:, c, :])),
              reads=[XB[c][t] for t in range(TOK // TM)], writes=(OUTB,), track=OUTB)
    S.barrier([OUTB], ("sp",))

    with nc.Block() as block:
        S.emit_all(block)
    stack.close()
    return nc


def _rope_tables(pos):
    inv = (10000.0 ** (-np.arange(0, 64, 2, dtype=np.float32) / 64)).astype(np.float32)
    ang = pos.astype(np.float32)[:, None] * inv[None, :]
    return np.cos(ang).astype(np.float32), np.sin(ang).astype(np.float32)


def _pc(v, nchunk):
    return np.ascontiguousarray(v.reshape(nchunk, P).T)


def prepare_inputs(inp, layers):
    f = np.float32
    lay_mlp = [l for l in layers] if STAGE >= 4 else []
    lay_even = [l for l in layers if l % 2 == 0] if STAGE >= 2 else []
    lay_odd = [l for l in layers if l % 2 == 1] if STAGE >= 2 else []
    dummy = np.zeros((1, 1, 1), f)
    x = inp["x"]
    c = inp["c"]
    common = {}
    common["n1g"] = np.concatenate([_pc(inp["norm1_g"][l], 16) for l in range(4)], axis=1).astype(f)
    common["n2g"] = np.concatenate([_pc(inp["norm2_g"][l], 16) for l in range(4)], axis=1).astype(f)
    common["cT"] = np.ascontiguousarray(c.T.reshape(16, P, 4).transpose(1, 0, 2).reshape(P, 64)).astype(f)
    common["w1"] = np.ascontiguousarray(inp["mlp_w1"][lay_mlp]) if lay_mlp else dummy
    common["w2"] = np.ascontiguousarray(inp["mlp_w2"][lay_mlp]) if lay_mlp else dummy
    ev = [l // 2 for l in lay_even]
    od = [l // 2 for l in lay_odd]
    common["abin"] = np.ascontiguousarray(inp["ab_w_in"][ev]) if ev else dummy
    common["about"] = np.ascontiguousarray(inp["ab_w_out"][ev]) if ev else dummy
    common["wmT"] = np.ascontiguousarray(inp["sgu_w"].transpose(3, 0, 1, 2).reshape(P, 2 * 1024)).astype(f)
    common["sgug"] = np.ascontiguousarray(np.broadcast_to(inp["sgu_norm_g"].reshape(1, 2 * 1024), (P, 2 * 1024))).astype(f)
    common["sgub"] = np.ascontiguousarray(np.broadcast_to(inp["sgu_b"].reshape(1, 2 * 1024), (P, 2 * 1024))).astype(f)
    common["convw"] = np.ascontiguousarray(inp["conv_w"].reshape(2, 31, 8, P).transpose(3, 0, 2, 1).reshape(P, 2 * 8 * 31)).astype(f)
    common["convb"] = np.concatenate([_pc(inp["conv_b"][e], 8) for e in range(2)], axis=1).astype(f)
    common["lng"] = np.concatenate([_pc(inp["conv_ln_g"][e], 8) for e in range(2)], axis=1).astype(f)
    common["lnb"] = np.concatenate([_pc(inp["conv_ln_b"][e], 8) for e in range(2)], axis=1).astype(f)
    perm = np.concatenate([np.arange(32, 64), np.arange(0, 32)])
    win = inp["mla_w_in"]
    uq = inp["mla_w_uq"].reshape(2, 512, 16, 192)
    common["mlain"] = np.ascontiguousarray(np.concatenate([win, win[:, :, 1024 + perm]], axis=2)[od]).astype(f) if od else dummy
    common["uq"] = np.ascontiguousarray(np.concatenate([uq, uq[:, :, :, 128 + perm]], axis=3).reshape(2, 512, 4096)[od]).astype(f) if od else dummy
    common["ukv"] = np.ascontiguousarray(inp["mla_w_ukv"][od]) if od else dummy
    common["mlaout"] = np.ascontiguousarray(inp["mla_w_out"][od]) if od else dummy
    common["qng"] = np.concatenate([_pc(inp["mla_q_norm_g"][o], 4) for o in range(2)], axis=1).astype(f)
    common["kvng"] = np.concatenate([_pc(inp["mla_kv_norm_g"][o], 4) for o in range(2)], axis=1).astype(f)
    qh, kh = inp["mla_q_head_g"], inp["mla_k_head_g"]
    common["qhgn"] = np.ascontiguousarray(qh[:, :128].T).astype(f)
    common["khgn"] = np.ascontiguousarray(kh[:, :128].T).astype(f)
    common["qhgr"] = np.ascontiguousarray(np.stack([qh[0, 128:], qh[0, 128 + perm], qh[1, 128:], qh[1, 128 + perm]], axis=1)).astype(f)
    common["khgr"] = np.ascontiguousarray(np.stack([kh[0, 128:], kh[0, 128 + perm], kh[1, 128:], kh[1, 128 + perm]], axis=1)).astype(f)
    kk = np.arange(P)
    common["tri"] = (kk[:, None] <= kk[None, :]).astype(f)
    common["trineg"] = np.where(kk[:, None] <= kk[None, :], 0.0, -1.0e5).astype(f)
    maps = []
    for r in range(NCORES):
        b, half = r // 2, r % 2
        m = dict(common)
        m["xT"] = np.ascontiguousarray(x[b, half * TOK:(half + 1) * TOK, :].T)
        m["adaw"] = np.ascontiguousarray(inp["ada_w"][:, :, r * 1536:(r + 1) * 1536])
        m["adab"] = np.ascontiguousarray(np.broadcast_to(inp["ada_b"][:, None, r * 1536:(r + 1) * 1536], (4, 4, 1536))).astype(f)
        oh = np.zeros((4, 1), f)
        oh[b, 0] = 1.0
        m["onehot"] = oh
        cos, sin = _rope_tables(np.arange(half * TOK, (half + 1) * TOK))
        m["cos2"] = np.ascontiguousarray(np.concatenate([cos.T, cos.T], axis=0))
        m["sin2s"] = np.ascontiguousarray(np.concatenate([-sin.T, sin.T], axis=0))
        m["rbias"] = np.full((P, 1), 0.0 if half == 1 else -30000.0, f)
        m["cmask"] = np.full((P, 1), 1.0 if half == 1 else 0.0, f)
        maps.append(m)
    return maps


_NC_CACHE = {}


def run_layers(inp, layers):
    key = tuple(layers)
    if key not in _NC_CACHE:
        _NC_CACHE[key] = build_program(list(layers))
    nc = _NC_CACHE[key]
    maps = prepare_inputs(inp, list(layers))
    res = run_bass_kernel_spmd(nc, maps, core_ids=list(range(NCORES)))
    out = np.empty((4, SEQ, D), np.float32)
    for r in range(NCORES):
        b, half = r // 2, r % 2
        out[b, half * TOK:(half + 1) * TOK, :] = res.results[r]["outT"].T
    return out


def kernel(**inputs):
    inp = {k: np.asarray(v) for k, v in inputs.items()}
    return run_layers(inp, [0, 1, 2, 3])
```

```python
import numpy as np
from contextlib import ExitStack
import concourse.bass as bass
import concourse.mybir as mybir
from concourse.bass_utils import run_bass_kernel_spmd

F32 = mybir.dt.float32
BF16 = mybir.dt.bfloat16
AF = mybir.ActivationFunctionType
ALU = mybir.AluOpType

P = 128
D = 2048
NCH = 16
TOK = 2048
SEQ = 4096
DFF = 8192
EPS = 1e-6
TM = 256
TF = 512
NCORES = 8
PAIRS = [[0, 1], [2, 3], [4, 5], [6, 7]]
ATTN_SCALE = 192 ** -0.5
SLOT_ELEMS = 4096
NSLOT = 3
import os
STAGE = int(os.environ.get('KSTAGE', '9'))
SUB = int(os.environ.get('KSUB', '9'))
GELU_F = AF.Identity if os.environ.get('KGELU') == '0' else AF.Gelu_apprx_tanh


class Buf:
    __slots__ = ("name", "w", "r", "dsem", "dcnt")

    def __init__(self, name):
        self.name = name
        self.w = None
        self.r = {}
        self.dsem = None
        self.dcnt = 0


class Eng:
    def __init__(self, name, sem):
        self.name = name
        self.sem = sem
        self.cnt = 0
        self.prog = []
        self.known = {}


class Sched:
    def __init__(self, nc, stack):
        self.nc = nc
        self.stack = stack
        self.nsem = 0
        self.sem_pool = []
        self.E = {n: Eng(n, self.new_sem("p_" + n)) for n in ("pe", "act", "dve", "pool", "sp")}

    def new_sem(self, name):
        self.nsem += 1
        return self.stack.enter_context(self.nc.semaphore(name + "_%d" % self.nsem))

    def _collect(self, E, reads, writes):
        need = {}

        def add(tok):
            if tok is None:
                return
            k = id(tok[0])
            if k not in need or need[k][1] < tok[1]:
                need[k] = tok

        for b in reads:
            add(b.w)
        for b in writes:
            add(b.w)
            for t in b.r.values():
                add(t)
        out = []
        for k, (sem, val) in need.items():
            if E.name == "pe" and sem is E.sem:
                continue
            if E.known.get(k, 0) >= val:
                continue
            E.known[k] = val
            out.append((sem, val))
        return out

    def op(self, eng, emit, reads=(), writes=()):
        E = self.E[eng]
        waits = self._collect(E, reads, writes)
        E.cnt += 1
        tok = (E.sem, E.cnt)
        E.prog.append((waits, emit, (E.sem, 1)))
        for b in reads:
            b.r[id(E.sem)] = tok
        for b in writes:
            b.w = tok
            b.r = {}

    def dma(self, q, emit, reads=(), writes=(), track=None, inc=16):
        E = self.E[q]
        waits = self._collect(E, reads, writes)
        tb = track
        if tb.dsem is None:
            if self.sem_pool:
                tb.dsem, tb.dcnt = self.sem_pool.pop()
            else:
                tb.dsem = self.new_sem("d")
        tb.dcnt += inc
        tok = (tb.dsem, tb.dcnt)
        E.prog.append((waits, emit, (tb.dsem, inc)))
        for b in reads:
            b.r[id(tb.dsem)] = tok
        for b in writes:
            b.w = tok
            b.r = {}

    def retire(self, bufs):
        for b in bufs:
            if b.dsem is not None:
                self.sem_pool.append((b.dsem, b.dcnt))
                b.dsem = None

    def fresh_progress(self):
        names = ("pe", "act", "dve", "pool", "sp")
        finals = [(self.E[n].sem, self.E[n].cnt) for n in names if self.E[n].cnt > 0]
        for n in names:
            E = self.E[n]
            waits = []
            for sem, val in finals:
                if sem is E.sem:
                    continue
                if E.known.get(id(sem), 0) >= val:
                    continue
                E.known[id(sem)] = val
                waits.append((sem, val))
            if waits:
                E.prog.append((waits, None, None))
        for n in names:
            E = self.E[n]
            E.sem = self.new_sem("p_" + n)
            E.cnt = 0

    def barrier(self, bufs, engines):
        for en in engines:
            E = self.E[en]
            waits = self._collect(E, (), bufs)
            if waits:
                E.prog.append((waits, None, None))

    def emit_all(self, block):
        def run(E, e):
            for waits, emit, inc in E.prog:
                for sem, val in waits:
                    e.wait_ge(sem, val)
                if emit is not None:
                    ins = emit(e)
                    ins.then_inc(inc[0], inc[1])

        @block.tensor
        def _(e):
            run(self.E["pe"], e)

        @block.scalar
        def _(e):
            run(self.E["act"], e)

        @block.vector
        def _(e):
            run(self.E["dve"], e)

        @block.gpsimd
        def _(e):
            run(self.E["pool"], e)

        @block.sync
        def _(e):
            run(self.E["sp"], e)


class Arena:
    def __init__(self, t, nelem):
        self.t = t
        self.n = nelem
        self.off = 0

    def reset(self):
        self.off = 0

    def alloc(self, free_shape, dt):
        n = int(np.prod(free_shape))
        ne = n * (2 if dt == F32 else 1)
        ne = (ne + 15) // 16 * 16
        assert self.off + ne <= self.n, ("arena overflow", self.off, ne, self.n)
        ap = self.t[:, self.off:self.off + ne]
        self.off += ne
        if dt == F32:
            ap = ap.bitcast(F32)
        ap = ap[:, 0:n]
        if len(free_shape) == 2:
            ap = ap.rearrange("p (a b) -> p a b", a=free_shape[0])
        elif len(free_shape) == 3:
            ap = ap.rearrange("p (a b c) -> p a b c", a=free_shape[0], b=free_shape[1])
        return ap


def build_program(layers):
    nc = bass.Bass("TRN2", target_bir_lowering=False)
    stack = ExitStack()
    dr = {}

    def din(name, shape, dt=F32):
        dr[name] = nc.dram_tensor(name, list(shape), dt, kind="ExternalInput").ap()
        return dr[name]

    xT_d = din("xT", [D, TOK])
    cT_d = din("cT", [P, 64])
    adaw_d = din("adaw", [4, D, 1536])
    adab_d = din("adab", [4, 4, 1536])
    onehot_d = din("onehot", [4, 1])
    n1g_d = din("n1g", [P, 64])
    n2g_d = din("n2g", [P, 64])
    lay_mlp = [l for l in layers] if STAGE >= 4 else []
    lay_even = [l for l in layers if l % 2 == 0] if STAGE >= 2 else []
    lay_odd = [l for l in layers if l % 2 == 1] if STAGE >= 2 else []
    w1_d = din("w1", [len(lay_mlp), D, DFF] if lay_mlp else [1, 1, 1])
    w2_d = din("w2", [len(lay_mlp), DFF, D] if lay_mlp else [1, 1, 1])
    abin_d = din("abin", [len(lay_even), D, 4096] if lay_even else [1, 1, 1])
    about_d = din("about", [len(lay_even), D, D] if lay_even else [1, 1, 1])
    wmT_d = din("wmT", [P, 2 * 1024])
    sgug_d = din("sgug", [P, 2 * 1024])
    sgub_d = din("sgub", [P, 2 * 1024])
    convw_d = din("convw", [P, 2 * 8 * 31])
    convb_d = din("convb", [P, 16])
    lng_d = din("lng", [P, 16])
    lnb_d = din("lnb", [P, 16])
    mlain_d = din("mlain", [len(lay_odd), D, 1152] if lay_odd else [1, 1, 1])
    uq_d = din("uq", [len(lay_odd), 512, 4096] if lay_odd else [1, 1, 1])
    ukv_d = din("ukv", [len(lay_odd), 512, 4096] if lay_odd else [1, 1, 1])
    mlaout_d = din("mlaout", [len(lay_odd), D, D] if lay_odd else [1, 1, 1])
    qng_d = din("qng", [P, 8])
    kvng_d = din("kvng", [P, 8])
    qhgn_d = din("qhgn", [P, 2])
    qhgr_d = din("qhgr", [64, 4])
    khgn_d = din("khgn", [P, 2])
    khgr_d = din("khgr", [64, 4])
    cos2_d = din("cos2", [64, TOK])
    sin2_d = din("sin2s", [64, TOK])
    tri_d = din("tri", [P, P])
    trineg_d = din("trineg", [P, P])
    rbias_d = din("rbias", [P, 1])
    cmask_d = din("cmask", [P, 1])
    outT_d = nc.dram_tensor("outT", [D, TOK], F32, kind="ExternalOutput").ap()

    modib = nc.dram_tensor("modib", [4, 6144], F32)
    modob = nc.dram_tensor("modob", [32, 6144], F32)
    haloib = {e: nc.dram_tensor("haloib%d" % e, [1024, 32], F32) for e in range(2)}
    haloob = {e: nc.dram_tensor("haloob%d" % e, [2048, 32], F32) for e in range(2)}
    kvib = {(o, t): nc.dram_tensor("kvib%d_%d" % (o, t), [576, TM], BF16) for o in range(2) for t in range(8)}
    kvob = {(o, t): nc.dram_tensor("kvob%d_%d" % (o, t), [1152, TM], BF16) for o in range(2) for t in range(8)}
    latloc = {o: nc.dram_tensor("latloc%d" % o, [576, TOK], BF16) for o in range(2)}
    cqscr = {o: nc.dram_tensor("cqscr%d" % o, [512, TOK], BF16) for o in range(2)}

    S = Sched(nc, stack)
    sb = lambda name, shape, dt: stack.enter_context(nc.sbuf_tensor(name, shape, dt))

    def bfcopy(name, src):
        return nc.dram_tensor(name + "_bf", list(src.shape), BF16).ap()
    w1_b, w2_b = bfcopy("w1", w1_d), bfcopy("w2", w2_d)
    abin_b, about_b = bfcopy("abin", abin_d), bfcopy("about", about_d)
    mlain_b, uq_b, ukv_b, mlaout_b = bfcopy("mlain", mlain_d), bfcopy("uq", uq_d), bfcopy("ukv", ukv_d), bfcopy("mlaout", mlaout_d)
    CV = {}

    def convert(name, src, dst, i):
        b = Buf("cv_%s%d" % (name, i))
        CV[(name, i)] = b
        R = src.shape[1]
        for j, r0 in enumerate(range(0, R, 128)):
            thr = []
            if j >= 4:
                tb = Buf("thr")
                tb.w = (b.dsem, b.dcnt - 48)
                thr = [tb]
            S.dma("pool", (lambda e, r0=r0: e.dma_start(out=dst[i, r0:r0 + 128, :], in_=src[i, r0:r0 + 128, :])),
                  reads=thr, writes=(b,), track=b)

    converted = set()

    def convert_layer(l):
        if l in converted or l not in layers:
            return
        converted.add(l)
        if STAGE >= 2:
            if l % 2 == 0:
                i = lay_even.index(l)
                convert("abin", abin_d, abin_b, i)
                convert("about", about_d, about_b, i)
            else:
                i = lay_odd.index(l)
                convert("mlain", mlain_d, mlain_b, i)
                convert("uq", uq_d, uq_b, i)
                convert("ukv", ukv_d, ukv_b, i)
                convert("mlaout", mlaout_d, mlaout_b, i)
        if l in lay_mlp:
            i = lay_mlp.index(l)
            convert("w1", w1_d, w1_b, i)
            convert("w2", w2_d, w2_b, i)

    def next_layer(l):
        idx = layers.index(l)
        return layers[idx + 1] if idx + 1 < len(layers) else None

    xT = sb("xT_sb", [P, NCH, TOK], F32)
    XB = [[Buf("x%d_%d" % (c, t)) for t in range(TOK // TM)] for c in range(NCH)]

    def xbufs(c, t0, n):
        return [XB[c][i] for i in range(t0 // TM, (t0 + n) // TM)]

    ARENA_N = 25600
    arena_t = sb("arena", [P, ARENA_N], BF16)
    AR = Arena(arena_t, ARENA_N)
    wsl_t = sb("wslots", [P, NSLOT, SLOT_ELEMS], BF16)
    WS = [Buf("ws%d" % i) for i in range(NSLOT)]
    ws_next = [0]
    ones_bf = sb("ones_bf", [P, P], BF16)
    tri_bf = sb("tri_bf", [P, P], BF16)
    modT = sb("modT", [P, 4 * 96], F32)
    a1 = sb("a1", [P, 64], F32)
    a2 = sb("a2", [P, 64], F32)
    n1g = sb("n1g_s", [P, 64], F32)
    n2g = sb("n2g_s", [P, 64], F32)
    smallc = sb("smallc", [P, 128], F32)
    CONST = Buf("const")
    MOD = Buf("mod")

    psum = [stack.enter_context(nc.psum_tensor("ps%d" % i, [P, 512], F32)) for i in range(8)]
    PSB = [Buf("psb%d" % i) for i in range(8)]
    ps_rr = [0]

    def next_ps(pool=(0, 1, 2, 3, 4, 5)):
        i = pool[ps_rr[0] % len(pool)]
        ps_rr[0] += 1
        return psum[i], PSB[i]

    extra_ws = []

    def next_ws():
        n = NSLOT + len(extra_ws)
        i = ws_next[0] % n
        ws_next[0] += 1
        if i < NSLOT:
            return wsl_t[:, i, :], WS[i]
        return extra_ws[i - NSLOT]

    def wload(dram_ap_fn, slot_ap, slot_buf, nsplit=2, q="pool", dep=()):
        for i in range(nsplit):
            dst, src = dram_ap_fn(i, nsplit)
            S.dma(q, (lambda e, dst=dst, src=src: e.dma_start(out=dst, in_=src)),
                  reads=dep, writes=(slot_buf,), track=slot_buf)

    def cload(dst, src):
        S.dma("sp", (lambda e, dst=dst, src=src: e.dma_start(out=dst, in_=src)), writes=(CONST,), track=CONST)

    cload(n1g[:], n1g_d[:, :])
    cload(n2g[:], n2g_d[:, :])
    cload(smallc[:, 0:1], rbias_d[:, :])
    cload(smallc[:, 1:2], cmask_d[:, :])
    cload(smallc[:, 2:4], qhgn_d[:, :])
    cload(smallc[:, 4:6], khgn_d[:, :])
    cload(smallc[:, 8:16], qng_d[:, :])
    cload(smallc[:, 16:24], kvng_d[:, :])
    cload(smallc[:, 24:40], convb_d[:, :])
    cload(smallc[:, 40:56], lng_d[:, :])
    cload(smallc[:, 56:72], lnb_d[:, :])
    cload(smallc[0:64, 72:76], qhgr_d[:, :])
    cload(smallc[0:64, 76:80], khgr_d[:, :])
    S.op("dve", lambda e: e.memset(ones_bf[:], 1.0), writes=(CONST,))
    S.op("dve", lambda e: e.memset(smallc[:, 127:128], EPS), writes=(CONST,))
    epsc = smallc[:, 127:128]
    rbias = smallc[:, 0:1]
    cmask = smallc[:, 1:2]

    xv = xT_d.rearrange("(c p) t -> p c t", p=P)
    XLOAD = Buf("xload")
    for c in range(NCH):
        S.dma("sp", (lambda e, c=c: e.dma_start(out=xT[:, c, :], in_=xv[:, c, :])),
              writes=[XB[c][t] for t in range(TOK // TM)], track=XLOAD)

    for c in range(NCH):
        for t in range(TOK // TM):
            XB[c][t].w = (XLOAD.dsem, XLOAD.dcnt)

    AR.reset()
    cT = AR.alloc([16, 4], F32)
    cact = AR.alloc([16, 4], BF16)
    modrow = AR.alloc([6144], F32)
    adab_s = AR.alloc([2, 1536], F32)
    ADB = [Buf("adab0"), Buf("adab1")]
    onehot = AR.alloc([1], F32)
    gath = AR.alloc([1, 1536], F32)
    PRO = Buf("pro")
    tri_f = AR.alloc([P], F32)
    TRIF = Buf("trif")
    S.dma("sp", lambda e: e.dma_start(out=tri_f, in_=tri_d[:, :]), writes=(TRIF,), track=TRIF)
    S.op("dve", lambda e: e.tensor_copy(out=tri_bf[:], in_=tri_f), reads=(TRIF,), writes=(CONST,))
    S.op("dve", lambda e: e.memset(smallc[:, 126:127], 0.0), writes=(CONST,))
    zeroc = smallc[:, 126:127]
    GATH = [Buf("gath0")]
    S.dma("sp", lambda e: e.dma_start(out=cT.rearrange("p a b -> p (a b)"), in_=cT_d[:, :]), writes=(PRO,), track=PRO)
    S.dma("sp", lambda e: e.dma_start(out=onehot[0:4, :], in_=onehot_d[:, :]), writes=(PRO,), track=PRO)
    CACT = Buf("cact")
    S.op("act", lambda e: e.activation(out=cact, in_=cT, func=AF.Silu), reads=(PRO,), writes=(CACT,))
    MODROW = Buf("modrow")
    for l in range(4):
        S.dma("sp", (lambda e, l=l: e.dma_start(out=adab_s[0:4, l % 2, :], in_=adab_d[l])), writes=(ADB[l % 2],), track=ADB[l % 2])
        for nt in range(3):
            pt, pb = next_ps()
            for half in range(2):
                sl, slb = next_ws()
                slv = sl.rearrange("p (k n) -> p k n", k=8)
                src = adaw_d[l].rearrange("(k p) n -> p k n", p=P)

                def mk(i, ns, slv=slv, src=src, half=half, nt=nt):
                    return (slv[:, 4 * i:4 * i + 4, :], src[:, half * 8 + 4 * i: half * 8 + 4 * i + 4, nt * 512:(nt + 1) * 512])
                wload(mk, sl, slb)

                def mm(e, slv=slv, half=half, pt=pt):
                    ins = None
                    for k in range(8):
                        kc = half * 8 + k
                        ins = e.matmul(pt[0:4, :], lhsT=cact[:, kc, :], rhs=slv[:, k, :], start=(kc == 0), stop=(kc == 15))
                    return ins
                S.op("pe", mm, reads=(CACT, slb), writes=(pb,))
            col = l * 1536 + nt * 512
            S.op("dve", (lambda e, pt=pt, col=col, l=l, nt=nt: e.tensor_tensor(out=modrow[0:4, col:col + 512], in0=pt[0:4, :],
                                                                   in1=adab_s[0:4, l % 2, nt * 512:(nt + 1) * 512], op=ALU.add)),
                 reads=(pb, ADB[l % 2]), writes=(MODROW,))
    MODIB = Buf("modib")
    MODOB = Buf("modob")
    S.dma("sp", lambda e: e.dma_start(out=modib[:, :], in_=modrow[0:4, :]), reads=(MODROW,), writes=(MODIB,), track=MODIB)
    S.dma("pool", lambda e: e.collective_compute("AllGather", ALU.bypass, replica_groups=[list(range(NCORES))],
                                                 ins=[modib.ap().opt()], outs=[modob.ap().opt()]),
          reads=(MODIB,), writes=(MODOB,), track=MODOB, inc=1)
    mps, mpb = next_ps()
    gi = 0
    modob_v = modob.ap().rearrange("(r b) (l n) -> b r l n", b=4, l=4)
    for l in range(4):
        for r in range(NCORES):
            g, gb = gath[:, 0, :], GATH[0]
            gi += 1
            S.dma("sp", (lambda e, g=g, r=r, l=l: e.dma_start(out=g[0:4, 0:1536], in_=modob_v[:, r, l, :])),
                  reads=(MODOB,), writes=(gb,), track=gb)

            def mm(e, g=g, r=r, l=l):
                ins = None
                for j in range(12):
                    colo = l * 96 + r * 12 + j
                    ins = e.matmul(mps[:, colo:colo + 1], lhsT=g[0:4, j * 128:(j + 1) * 128], rhs=onehot[0:4, 0:1],
                                   start=True, stop=True)
                return ins
            S.op("pe", mm, reads=(gb, PRO), writes=(mpb,))
    S.op("dve", lambda e: e.tensor_copy(out=modT[:], in_=mps[:, 0:384]), reads=(mpb,), writes=(MOD,))
    for l in range(4):
        S.op("dve", (lambda e, l=l: e.scalar_tensor_tensor(out=a1[:, l * 16:(l + 1) * 16], in0=modT[:, l * 96 + 16:l * 96 + 32],
                                                           scalar=1.0, in1=n1g[:, l * 16:(l + 1) * 16], op0=ALU.add, op1=ALU.mult)),
             reads=(MOD, CONST), writes=(MOD,))
        S.op("dve", (lambda e, l=l: e.scalar_tensor_tensor(out=a2[:, l * 16:(l + 1) * 16], in0=modT[:, l * 96 + 64:l * 96 + 80],
                                                           scalar=1.0, in1=n2g[:, l * 16:(l + 1) * 16], op0=ALU.add, op1=ALU.mult)),
             reads=(MOD, CONST), writes=(MOD,))
    phase_bufs = [PRO, CACT, MODROW, TRIF] + GATH + ADB
    if layers and STAGE >= 2:
        convert_layer(layers[0])

    def modcol(l, seg, c):
        return modT[:, l * 96 + seg * 16 + c: l * 96 + seg * 16 + c + 1]

    ALLENG = ("pe", "act", "dve", "sp")

    def new_phase(bufs):
        del extra_ws[:]
        S.barrier(bufs, ALLENG + ("pool",))
        S.retire(bufs)
        if max(E.cnt for E in S.E.values()) > 12000:
            S.fresh_progress()
        AR.reset()

    def rsqrt(out, in_, scale, rbufs, wbuf):
        npart = out.partition_size()
        S.op("act", (lambda e: e.activation(out=out, in_=in_, func=AF.Sqrt, bias=epsc[0:npart, :], scale=scale)),
             reads=list(rbufs) + [CONST], writes=(wbuf,))
        S.op("dve", (lambda e: e.reciprocal(out=out, in_=out)), reads=(wbuf,), writes=(wbuf,))

    def rms_modulate(l, which, t0, T, h, HB, sqs, SQB, rstd, RSB, tmp, TMPB):
        aa = a1 if which == 1 else a2
        seg_shift = 0 if which == 1 else 3
        pt, pb = next_ps()
        for c in range(NCH):
            q, qb = sqs[:, c % 2, 0:T], SQB[c % 2]
            S.op("act", (lambda e, c=c, q=q: e.activation(out=q, in_=xT[:, c, t0:t0 + T], func=AF.Square)),
                 reads=xbufs(c, t0, T), writes=(qb,))
            S.op("pe", (lambda e, c=c, q=q: e.matmul(pt[:, 0:T], lhsT=ones_bf[:], rhs=q, start=(c == 0), stop=(c == NCH - 1))),
                 reads=(qb, CONST), writes=(pb,))
        rsqrt(rstd[:, 0:T], pt[:, 0:T], 1.0 / D, (pb,), RSB)
        for c in range(NCH):
            tp, tb = tmp[:, c % 2, 0:T], TMPB[c % 2]
            S.op("dve", (lambda e, c=c, tp=tp: e.scalar_tensor_tensor(out=tp, in0=xT[:, c, t0:t0 + T], scalar=aa[:, l * 16 + c:l * 16 + c + 1],
                                                                      in1=rstd[:, 0:T], op0=ALU.mult, op1=ALU.mult)),
                 reads=xbufs(c, t0, T) + [RSB, MOD], writes=(tb,))
            S.op("act", (lambda e, c=c, tp=tp: e.activation(out=h[:, c, 0:T], in_=tp, func=AF.Identity, bias=modcol(l, seg_shift, c), scale=1.0)),
                 reads=(tb, MOD), writes=(HB[c],))

    def resid_update(l, seg_gate, oc, pt, pb, t0, T):
        S.op("dve", (lambda e: e.scalar_tensor_tensor(out=xT[:, oc, t0:t0 + T], in0=pt[:, 0:T], scalar=modcol(l, seg_gate, oc),
                                                      in1=xT[:, oc, t0:t0 + T], op0=ALU.mult, op1=ALU.add)),
             reads=[pb, MOD] + xbufs(oc, t0, T), writes=xbufs(oc, t0, T))

    def mlp_layer(l):
        nonlocal phase_bufs
        new_phase(phase_bufs)
        h = AR.alloc([NCH, TF], BF16)
        HB = [Buf("h%d" % c) for c in range(NCH)]
        sqs = AR.alloc([2, TF], BF16)
        SQB = [Buf("sq0"), Buf("sq1")]
        rstd = AR.alloc([TF], F32)
        RSB = Buf("rstd")
        tmp = AR.alloc([2, TF], F32)
        TMPB = [Buf("tmp0"), Buf("tmp1")]
        hid = AR.alloc([2, 4, TF], BF16)
        HIDB = [[Buf("hid%d_%d" % (i, j)) for j in range(4)] for i in range(2)]
        rf = AR.alloc([2, TF], F32)
        RFB = [Buf("rf0"), Buf("rf1")]
        phase_bufs = HB + SQB + [RSB] + TMPB + [b for r in HIDB for b in r] + RFB
        xs = AR.alloc([SLOT_ELEMS], BF16)
        XSB = Buf("xslot")
        extra_ws.append((xs, XSB))
        phase_bufs.append(XSB)
        w1v = w1_b[lay_mlp.index(l)].rearrange("(k p) n -> p k n", p=P)
        w2v = w2_b[lay_mlp.index(l)].rearrange("(k p) n -> p k n", p=P)
        for tt in range(TOK // TF):
            t0 = tt * TF
            rms_modulate(l, 2, t0, TF, h, HB, sqs, SQB, rstd, RSB, tmp, TMPB)
            for sbk in range(DFF // 512):
                hs = sbk % 2
                for wb in range(2):
                    sl, slb = next_ws()
                    slv = sl.rearrange("p (k n) -> p k n", k=16)
                    c0 = sbk * 512 + wb * 256

                    def mk(i, ns, slv=slv, c0=c0):
                        return (slv[:, 8 * i:8 * i + 8, :], w1v[:, 8 * i:8 * i + 8, c0:c0 + 256])
                    wload(mk, sl, slb, q="sp", dep=(CV[("w1", lay_mlp.index(l))],))
                    for j in range(2):
                        hc = wb * 2 + j
                        pt, pb = next_ps()

                        def mm(e, slv=slv, j=j, pt=pt):
                            ins = None
                            for kc in range(NCH):
                                ins = e.matmul(pt[:, 0:TF], lhsT=slv[:, kc, j * 128:(j + 1) * 128], rhs=h[:, kc, :],
                                               start=(kc == 0), stop=(kc == NCH - 1))
                            return ins
                        S.op("pe", mm, reads=HB + [slb], writes=(pb,))
                        r_, rb = rf[:, hc % 2, :], RFB[hc % 2]
                        S.op("act", (lambda e, pt=pt, r_=r_: e.activation(out=r_, in_=pt[:, 0:TF], func=AF.Relu)),
                             reads=(pb,), writes=(rb,))
                        S.op("dve", (lambda e, r_=r_, hs=hs, hc=hc: e.tensor_tensor(out=hid[:, hs, hc, :], in0=r_, in1=r_, op=ALU.mult)),
                             reads=(rb,), writes=(HIDB[hs][hc],))
                w2s = []
                for wb in range(2):
                    sl, slb = next_ws()
                    slv = sl.rearrange("p (k n) -> p k n", k=2)
                    k0 = sbk * 4 + wb * 2

                    def mk(i, ns, slv=slv, k0=k0):
                        return (slv[:, i:i + 1, :], w2v[:, k0 + i:k0 + i + 1, :])
                    wload(mk, sl, slb, q="sp", dep=(CV[("w2", lay_mlp.index(l))],))
                    w2s.append((slv, slb))
                for oc in range(NCH):
                    pt, pb = next_ps()

                    def mm(e, oc=oc, pt=pt, w2s=w2s, hs=hs):
                        ins = None
                        for hc in range(4):
                            slv = w2s[hc // 2][0]
                            ins = e.matmul(pt[:, 0:TF], lhsT=slv[:, hc % 2, oc * 128:(oc + 1) * 128], rhs=hid[:, hs, hc, :],
                                           start=(hc == 0), stop=(hc == 3))
                        return ins
                    S.op("pe", mm, reads=HIDB[hs] + [w2s[0][1], w2s[1][1]], writes=(pb,))
                    resid_update(l, 5, oc, pt, pb, t0, TF)

    def even_mixer(l):
        nonlocal phase_bufs
        e_ = l // 2
        new_phase(phase_bufs)
        h = AR.alloc([NCH, TM], BF16)
        HB = [Buf("h%d" % c) for c in range(NCH)]
        sqs = AR.alloc([2, TM], BF16)
        SQB = [Buf("sq0"), Buf("sq1")]
        rstd = AR.alloc([TM], F32)
        RSB = Buf("rstd")
        tmp = AR.alloc([2, TM], F32)
        TMPB = [Buf("tmp0"), Buf("tmp1")]
        cat = AR.alloc([8, TM], BF16)
        CATB = [Buf("cat%d" % c) for c in range(8)]
        gv = AR.alloc([2, 1024], F32)
        GVB = [Buf("gv0"), Buf("gv1")]
        vn = AR.alloc([2, 1024], BF16)
        VNB = [Buf("vn0"), Buf("vn1")]
        vst = AR.alloc([2, 16], F32)
        VSTB = [Buf("vst0"), Buf("vst1")]
        ybuf = AR.alloc([2, 30 + TM], F32)
        YB = [Buf("y0"), Buf("y1")]
        tails = AR.alloc([8, 32], F32)
        TAILB = [Buf("tail%d" % c) for c in range(8)]
        acc = AR.alloc([2, TM], F32)
        ACCB = [Buf("acc0"), Buf("acc1")]
        sg = AR.alloc([2, TM], F32)
        SGB = [Buf("sg0"), Buf("sg1")]
        lnst = AR.alloc([3, TM], F32)
        LNB = Buf("lnst")
        wm_bf = AR.alloc([8, P], BF16)
        wm_f = gv[:, 0, :].rearrange("p (a b) -> p a b", a=8)
        sgug = AR.alloc([1024], F32)
        sgub = AR.alloc([1024], F32)
        convw = AR.alloc([8, 31], F32)
        EC = Buf("evenconst")
        phase_bufs = HB + SQB + [RSB] + TMPB + CATB + GVB + VNB + VSTB + YB + TAILB + ACCB + SGB + [LNB, EC]
        S.dma("sp", lambda e: e.dma_start(out=wm_f.rearrange("p a b -> p (a b)"), in_=wmT_d[:, e_ * 1024:(e_ + 1) * 1024]), writes=(GVB[0],), track=GVB[0])
        S.dma("sp", lambda e: e.dma_start(out=sgug, in_=sgug_d[:, e_ * 1024:(e_ + 1) * 1024]), writes=(EC,), track=EC)
        S.dma("sp", lambda e: e.dma_start(out=sgub, in_=sgub_d[:, e_ * 1024:(e_ + 1) * 1024]), writes=(EC,), track=EC)
        S.dma("sp", lambda e: e.dma_start(out=convw.rearrange("p a b -> p (a b)"), in_=convw_d[:, e_ * 248:(e_ + 1) * 248]), writes=(EC,), track=EC)
        WMB = Buf("wm")
        phase_bufs.append(WMB)
        S.op("dve", lambda e: e.tensor_tensor(out=wm_bf, in0=wm_f, in1=tri_bf[:].unsqueeze(1).to_broadcast([P, 8, P]), op=ALU.mult),
             reads=(GVB[0], CONST), writes=(WMB,))
        convb = smallc[:, 24 + e_ * 8: 32 + e_ * 8]
        lng = smallc[:, 40 + e_ * 8: 48 + e_ * 8]
        lnb = smallc[:, 56 + e_ * 8: 64 + e_ * 8]
        winv = abin_b[lay_even.index(l)].rearrange("(k p) n -> p k n", p=P)
        woutv = about_b[lay_even.index(l)].rearrange("(k p) n -> p k n", p=P)

        def load_win(c0, ncols=256):
            sl, slb = next_ws()
            slv = sl.rearrange("p (k n) -> p k n", k=16)

            def mk(i, ns, slv=slv, c0=c0):
                return (slv[:, 8 * i:8 * i + 8, 0:ncols], winv[:, 8 * i:8 * i + 8, c0:c0 + ncols])
            wload(mk, sl, slb, q="sp", dep=(CV[("abin", lay_even.index(l))],))
            return slv, slb

        def proj_fm(slv, slb, j, T, pool=(0, 1, 2, 3, 4, 5)):
            pt, pb = next_ps(pool)

            def mm(e, slv=slv, j=j, pt=pt):
                ins = None
                for kc in range(NCH):
                    ins = e.matmul(pt[:, 0:T], lhsT=slv[:, kc, j * 128:(j + 1) * 128], rhs=h[:, kc, 0:T],
                                   start=(kc == 0), stop=(kc == NCH - 1))
                return ins
            S.op("pe", mm, reads=HB + [slb], writes=(pb,))
            return pt, pb

        GC1 = 1.0 / 0.044715
        GC2 = 2.0 * 0.7978845608028654 * 0.044715

        def gelu(src, srcb, T, out, outb, k):
            a, ab = acc[:, k % 2, 0:T], ACCB[k % 2]
            s_, s_b = sg[:, k % 2, 0:T], SGB[k % 2]
            S.op("act", (lambda e: e.activation(out=a, in_=src, func=AF.Square)), reads=(srcb,), writes=(ab,))
            S.op("dve", (lambda e: e.scalar_tensor_tensor(out=a, in0=a, scalar=GC1, in1=src, op0=ALU.add, op1=ALU.mult)), reads=(ab, srcb), writes=(ab,))
            S.op("act", (lambda e: e.activation(out=s_, in_=a, func=AF.Sigmoid, scale=GC2)), reads=(ab,), writes=(s_b,))
            S.op("dve", (lambda e: e.tensor_tensor(out=out, in0=src, in1=s_, op=ALU.mult)), reads=(srcb, s_b), writes=(outb,))

        def y_from_ag(c, T, ysl, ysb, pa, pab, pg, pgb, off):
            s_, s_b = sg[:, c % 2, 0:T], SGB[c % 2]
            S.op("act", (lambda e: e.activation(out=s_, in_=pg[:, 0:T], func=AF.Sigmoid)), reads=(pgb,), writes=(s_b,))
            S.op("dve", (lambda e: e.tensor_tensor(out=ysl[:, off:off + T], in0=pa[:, 0:T], in1=s_, op=ALU.mult)),
                 reads=(pab, s_b), writes=(ysb,))

        HIB, HOB = Buf("haloib"), Buf("haloob")
        t0 = TOK - 32
        rms_modulate(l, 1, t0, 32, h, HB, sqs, SQB, rstd, RSB, tmp, TMPB)
        for c in range(8):
            if c % 2 == 0:
                sa, sab = load_win(2048 + (c // 2) * 256)
                sgl, sglb = load_win(3072 + (c // 2) * 256)
            pa, pab = proj_fm(sa, sab, c % 2, 32)
            pg, pgb = proj_fm(sgl, sglb, c % 2, 32)
            ysl, ysb = ybuf[:, c % 2, :], YB[c % 2]
            y_from_ag(c, 32, ysl, ysb, pa, pab, pg, pgb, 0)
            S.dma("sp", (lambda e, c=c, ysl=ysl: e.dma_start(out=haloib[e_][c * 128:(c + 1) * 128, :], in_=ysl[:, 0:32])),
                  reads=(ysb,), writes=(HIB,), track=HIB)
        S.dma("pool", lambda e: e.collective_compute("AllGather", ALU.bypass, replica_groups=PAIRS,
                                                     ins=[haloib[e_].ap().opt()], outs=[haloob[e_].ap().opt()]),
              reads=(HIB,), writes=(HOB,), track=HOB, inc=1)
        if next_layer(l) is not None:
            convert_layer(next_layer(l))
        for c in range(8):
            S.dma("sp", (lambda e, c=c: e.dma_start(out=tails[:, c, :], in_=haloob[e_][c * 128:(c + 1) * 128, :])),
                  reads=(HOB,), writes=(TAILB[c],), track=TAILB[c])
            S.op("dve", (lambda e, c=c: e.tensor_scalar(out=tails[:, c, :], in0=tails[:, c, :], scalar1=cmask, scalar2=None, op0=ALU.mult)),
                 reads=(TAILB[c], CONST), writes=(TAILB[c],))

        for tt in range(TOK // TM):
            if STAGE == 2:
                break
            t0 = tt * TM
            rms_modulate(l, 1, t0, TM, h, HB, sqs, SQB, rstd, RSB, tmp, TMPB)
            for j4 in range(4):
                slv, slb = load_win(j4 * 256)
                for j in range(2):
                    g = j4 * 2 + j
                    pt, pb = proj_fm(slv, slb, j, TM)
                    gelu(pt[:, 0:TM], pb, TM, cat[:, g, :], CATB[g], g)
            if SUB == 1:
                break
            for j4 in range(4):
                slv, slb = load_win(1024 + j4 * 256)
                for tc in range(2):
                    pt, pb = next_ps()

                    def mm(e, slv=slv, tc=tc, pt=pt):
                        ins = None
                        for kc in range(NCH):
                            ins = e.matmul(pt[:, 0:256], lhsT=h[:, kc, tc * 128:(tc + 1) * 128], rhs=slv[:, kc, :],
                                           start=(kc == 0), stop=(kc == NCH - 1))
                        return ins
                    S.op("pe", mm, reads=HB + [slb], writes=(pb,))
                    gelu(pt[:, 0:256], pb, 256, gv[:, tc, j4 * 256:(j4 + 1) * 256], GVB[tc], tc)
            if SUB == 11:
                break
            for tc in range(2):
                S.op("act", (lambda e, tc=tc: e.activation(out=vn[:, tc, :], in_=gv[:, tc, :], func=AF.Square)),
                     reads=(GVB[tc],), writes=(VNB[tc],))
                S.op("dve", (lambda e, tc=tc: e.tensor_reduce(out=vst[:, tc, 0:8], in_=vn[:, tc, :].rearrange("p (g d) -> p g d", g=8),
                                                              axis=mybir.AxisListType.X, op=ALU.add)),
                     reads=(VNB[tc],), writes=(VSTB[tc],))
                rsqrt(vst[:, tc, 8:16], vst[:, tc, 0:8], 1.0 / 128, (VSTB[tc],), VSTB[tc])
                if SUB == 12:
                    continue
                for g in range(8):
                    S.op("dve", (lambda e, tc=tc, g=g: e.scalar_tensor_tensor(out=vn[:, tc, g * 128:(g + 1) * 128], in0=gv[:, tc, g * 128:(g + 1) * 128],
                                                                              scalar=vst[:, tc, 8 + g:9 + g], in1=sgug[:, g * 128:(g + 1) * 128],
                                                                              op0=ALU.mult, op1=ALU.mult)),
                         reads=(GVB[tc], VSTB[tc], EC), writes=(VNB[tc],))
                if SUB == 13:
                    continue
                for half in range(2):
                    pt, pb = next_ps()

                    def mm(e, tc=tc, half=half, pt=pt):
                        ins = None
                        for gg in range(4):
                            g = half * 4 + gg
                            ins = e.matmul(pt[:, gg * 128:(gg + 1) * 128], lhsT=vn[:, tc, g * 128:(g + 1) * 128], rhs=wm_bf[:, g, :],
                                           start=True, stop=True)
                        return ins
                    S.op("pe", mm, reads=(VNB[tc], WMB), writes=(pb,))
                    mixt, mixb = (acc if half == 0 else sg), (ACCB if half == 0 else SGB)
                    mv = mixt.rearrange("p a b -> p (a b)")
                    S.op("dve", (lambda e, pt=pt, half=half, mv=mv: e.tensor_tensor(out=mv, in0=pt[:, 0:512], in1=sgub[:, half * 512:(half + 1) * 512], op=ALU.add)),
                         reads=(pb, EC), writes=mixb)
                    for gg in range(4):
                        g = half * 4 + gg
                        S.op("dve", (lambda e, g=g, gg=gg, tc=tc, mv=mv: e.tensor_tensor(out=cat[:, g, tc * 128:(tc + 1) * 128], in0=mv[:, gg * 128:(gg + 1) * 128],
                                                                                     in1=cat[:, g, tc * 128:(tc + 1) * 128], op=ALU.mult)),
                             reads=list(mixb) + [CATB[g]], writes=(CATB[g],))
            if SUB in (2, 12, 13):
                break
            out_proj_half(l, woutv, 0, cat, CATB, t0)
            if SUB == 3:
                break
            s1p, s1b = psum[6], PSB[6]
            s2p, s2b = psum[7], PSB[7]
            for c in range(8):
                if c % 2 == 0:
                    sa, sab = load_win(2048 + (c // 2) * 256)
                    sgl, sglb = load_win(3072 + (c // 2) * 256)
                pa, pab = proj_fm(sa, sab, c % 2, TM)
                pg, pgb = proj_fm(sgl, sglb, c % 2, TM)
                ysl, ysb = ybuf[:, c % 2, :], YB[c % 2]
                S.op("act", (lambda e, c=c, ysl=ysl: e.activation(out=ysl[:, 0:30], in_=tails[:, c, 2:32], func=AF.Copy)),
                     reads=(TAILB[c],), writes=(ysb,))
                y_from_ag(c, TM, ysl, ysb, pa, pab, pg, pgb, 30)
                S.op("act", (lambda e, c=c, ysl=ysl: e.activation(out=tails[:, c, 2:32], in_=ysl[:, TM:TM + 30], func=AF.Copy)),
                     reads=(ysb,), writes=(TAILB[c],))
                ac, acb = acc[:, c % 2, :], ACCB[c % 2]

                def conv(e, c=c, ysl=ysl, ac=ac):
                    ins = e.tensor_scalar(out=ac, in0=ysl[:, 0:TM], scalar1=convw[:, c, 0:1], scalar2=convb[:, c:c + 1], op0=ALU.mult, op1=ALU.add)
                    return ins
                S.op("dve", conv, reads=(ysb, EC, CONST), writes=(acb,))
                for j in range(1, 31):
                    S.op("dve", (lambda e, c=c, ysl=ysl, ac=ac, j=j: e.scalar_tensor_tensor(out=ac, in0=ysl[:, j:j + TM], scalar=convw[:, c, j:j + 1],
                                                                                         in1=ac, op0=ALU.mult, op1=ALU.add)),
                         reads=(ysb, EC, acb), writes=(acb,))
                S.op("act", (lambda e, c=c, ac=ac: e.activation(out=cat[:, c, :], in_=ac, func=AF.Copy)), reads=(acb,), writes=(CATB[c],))
                q, qb = sqs[:, c % 2, :], SQB[c % 2]
                S.op("act", (lambda e, q=q, ac=ac: e.activation(out=q, in_=ac, func=AF.Square)), reads=(acb,), writes=(qb,))
                S.op("pe", (lambda e, c=c: e.matmul(s1p[:, 0:TM], lhsT=ones_bf[:], rhs=cat[:, c, :], start=(c == 0), stop=(c == 7))),
                     reads=(CATB[c], CONST), writes=(s1b,))
                S.op("pe", (lambda e, c=c, q=q: e.matmul(s2p[:, 0:TM], lhsT=ones_bf[:], rhs=q, start=(c == 0), stop=(c == 7))),
                     reads=(qb, CONST), writes=(s2b,))
            if SUB == 4:
                break
            mean, rs, nmr = lnst[:, 0, :], lnst[:, 1, :], lnst[:, 2, :]
            S.op("dve", lambda e: e.tensor_scalar(out=mean, in0=s1p[:, 0:TM], scalar1=1.0 / 1024, scalar2=None, op0=ALU.mult), reads=(s1b,), writes=(LNB,))
            S.op("dve", lambda e: e.tensor_tensor(out=nmr, in0=mean, in1=mean, op=ALU.mult), reads=(LNB,), writes=(LNB,))
            S.op("dve", lambda e: e.scalar_tensor_tensor(out=rs, in0=s2p[:, 0:TM], scalar=1.0 / 1024, in1=nmr, op0=ALU.mult, op1=ALU.subtract),
                 reads=(s2b, LNB), writes=(LNB,))
            rsqrt(rs, rs, 1.0, (LNB,), LNB)
            S.op("dve", lambda e: e.scalar_tensor_tensor(out=nmr, in0=mean, scalar=-1.0, in1=rs, op0=ALU.mult, op1=ALU.mult), reads=(LNB,), writes=(LNB,))
            for c in range(8):
                tp, tb = tmp[:, c % 2, :], TMPB[c % 2]
                S.op("dve", (lambda e, c=c, tp=tp: e.tensor_tensor(out=tp, in0=rs, in1=cat[:, c, :], op=ALU.mult)), reads=(CATB[c], LNB), writes=(tb,))
                S.op("dve", (lambda e, c=c, tp=tp: e.tensor_tensor(out=tp, in0=tp, in1=nmr, op=ALU.add)), reads=(tb, LNB), writes=(tb,))
                S.op("act", (lambda e, c=c, tp=tp: e.activation(out=cat[:, c, :], in_=tp, func=AF.Silu, bias=lnb[:, c:c + 1], scale=lng[:, c:c + 1])),
                     reads=(tb, CONST), writes=(CATB[c],))
            out_proj_half(l, woutv, 1, cat, CATB, t0)

    def out_proj_half(l, woutv, part, cat, CATB, t0):
        for q4 in range(4):
            sl, slb = next_ws()
            slv = sl.rearrange("p (k n) -> p k n", k=8)

            def mk(i, ns, slv=slv, q4=q4):
                return (slv[:, 4 * i:4 * i + 4, :], woutv[:, part * 8 + 4 * i: part * 8 + 4 * i + 4, q4 * 512:(q4 + 1) * 512])
            wload(mk, sl, slb, q="sp", dep=(CV[("about", lay_even.index(l))],))
            for j in range(4):
                oc = q4 * 4 + j
                pt, pb = next_ps()

                def mm(e, slv=slv, j=j, pt=pt):
                    ins = None
                    for kc in range(8):
                        ins = e.matmul(pt[:, 0:TM], lhsT=slv[:, kc, j * 128:(j + 1) * 128], rhs=cat[:, kc, :], start=(kc == 0), stop=(kc == 7))
                    return ins
                S.op("pe", mm, reads=CATB + [slb], writes=(pb,))
                resid_update(l, 2, oc, pt, pb, t0, TM)


    def proj_chunk(slv, slb, h, HB, col0, ncols, T, nk=NCH, pool=(0, 1, 2, 3, 4, 5)):
        pt, pb = next_ps(pool)

        def mm(e):
            ins = None
            for kc in range(nk):
                ins = e.matmul(pt[0:ncols, 0:T], lhsT=slv[:, kc, col0:col0 + ncols], rhs=h[:, kc, 0:T], start=(kc == 0), stop=(kc == nk - 1))
            return ins
        S.op("pe", mm, reads=list(HB) + [slb], writes=(pb,))
        return pt, pb

    def sumsq(src, srcb, nrow, T, sqs, SQB, k):
        q, qb = sqs[0:nrow, k % 2, 0:T], SQB[k % 2]
        S.op("act", (lambda e: e.activation(out=q, in_=src, func=AF.Square)), reads=(srcb,), writes=(qb,))
        pt, pb = next_ps()
        S.op("pe", (lambda e: e.matmul(pt[0:nrow, 0:T], lhsT=ones_bf[0:nrow, 0:nrow], rhs=q, start=True, stop=True)), reads=(qb, CONST), writes=(pb,))
        return pt, pb

    def rope_combine(pa, pab, pb2, pbb, g0, g1, rs2, RS2B, cs, sn, TABB, ta, TAB, tb2, TBB, out, outb, T):
        S.op("dve", (lambda e: e.scalar_tensor_tensor(out=ta[0:64, 0:T], in0=pa[0:64, 0:T], scalar=g0, in1=rs2[0:64, 0:T], op0=ALU.mult, op1=ALU.mult)),
             reads=(pab, RS2B, CONST), writes=(TAB,))
        S.op("dve", (lambda e: e.tensor_tensor(out=ta[0:64, 0:T], in0=ta[0:64, 0:T], in1=cs[0:64, 0:T], op=ALU.mult)), reads=(TAB, TABB), writes=(TAB,))
        S.op("dve", (lambda e: e.scalar_tensor_tensor(out=tb2[0:64, 0:T], in0=pb2[0:64, 0:T], scalar=g1, in1=rs2[0:64, 0:T], op0=ALU.mult, op1=ALU.mult)),
             reads=(pbb, RS2B, CONST), writes=(TBB,))
        S.op("dve", (lambda e: e.tensor_tensor(out=tb2[0:64, 0:T], in0=tb2[0:64, 0:T], in1=sn[0:64, 0:T], op=ALU.mult)), reads=(TBB, TABB), writes=(TBB,))
        S.op("dve", (lambda e: e.tensor_tensor(out=out, in0=ta[0:64, 0:T], in1=tb2[0:64, 0:T], op=ALU.add)), reads=(TAB, TBB), writes=(outb,))

    def mla_mixer(l):
        nonlocal phase_bufs
        o = l // 2
        oi = lay_odd.index(l)
        qng = smallc[:, 8 + o * 4: 12 + o * 4]
        kvng = smallc[:, 16 + o * 4: 20 + o * 4]
        qhgn = smallc[:, 2 + o: 3 + o]
        khgn = smallc[:, 4 + o: 5 + o]
        qhgr = smallc[0:64, 72 + 2 * o: 74 + 2 * o]
        khgr = smallc[0:64, 76 + 2 * o: 78 + 2 * o]
        new_phase(phase_bufs)
        h = AR.alloc([NCH, TM], BF16)
        HB = [Buf("h%d" % c) for c in range(NCH)]
        sqs = AR.alloc([2, TM], BF16)
        SQB = [Buf("sq0"), Buf("sq1")]
        rstd = AR.alloc([TM], F32)
        RSB = Buf("rstd")
        tmp = AR.alloc([2, TM], F32)
        TMPB = [Buf("tmp0"), Buf("tmp1")]
        cf = AR.alloc([4, TM], F32)
        CFB = [Buf("cf%d" % c) for c in range(4)]
        cn = AR.alloc([2, 4, TM], BF16)
        CNB = [Buf("cn0"), Buf("cn1")]
        kr = AR.alloc([TM], BF16)
        KRB = Buf("kr")
        cs = AR.alloc([TM], F32)
        sn = AR.alloc([TM], F32)
        TABB = Buf("tab")
        ta = AR.alloc([TM], F32)
        TAB = Buf("ta")
        tb2 = AR.alloc([TM], F32)
        TBB = Buf("tb")
        rs2 = AR.alloc([TM], F32)
        RS2B = Buf("rs2")
        phase_bufs = HB + SQB + [RSB] + TMPB + CFB + CNB + [KRB, TABB, TAB, TBB, RS2B]
        winv = mlain_b[oi].rearrange("(k p) n -> p k n", p=P)
        LATL = Buf("latloc%d" % o)
        CQS = Buf("cqscr%d" % o)
        KVOB = [Buf("kvob%d_%d" % (o, t)) for t in range(8)]

        def load_win(c0, ncols):
            sl, slb = next_ws()
            slv = sl.rearrange("p (k n) -> p k n", k=16)

            def mk(i, ns):
                return (slv[:, 8 * i:8 * i + 8, 0:ncols], winv[:, 8 * i:8 * i + 8, c0:c0 + ncols])
            wload(mk, sl, slb, q="sp", dep=(CV[("mlain", oi)],))
            return slv, slb

        for tt in range(TOK // TM):
            t0 = tt * TM
            KVIB = Buf("kvib%d_%d" % (o, tt))
            rms_modulate(l, 1, t0, TM, h, HB, sqs, SQB, rstd, RSB, tmp, TMPB)
            S.dma("sp", (lambda e, t0=t0: e.dma_start(out=cs[0:64, :], in_=cos2_d[:, t0:t0 + TM])), writes=(TABB,), track=TABB)
            S.dma("sp", (lambda e, t0=t0: e.dma_start(out=sn[0:64, :], in_=sin2_d[:, t0:t0 + TM])), writes=(TABB,), track=TABB)
            for which in range(2):
                ssp, ssb = psum[6], PSB[6]
                for half in range(2):
                    slv, slb = load_win(which * 512 + half * 256, 256)
                    for j in range(2):
                        c = half * 2 + j
                        pt, pb = proj_chunk(slv, slb, h, HB, j * 128, 128, TM)
                        S.op("act", (lambda e, pt=pt, c=c: e.activation(out=cf[:, c, :], in_=pt[:, 0:TM], func=AF.Copy)), reads=(pb,), writes=(CFB[c],))
                        q, qb = sqs[:, c % 2, :], SQB[c % 2]
                        S.op("act", (lambda e, q=q, c=c: e.activation(out=q, in_=cf[:, c, :], func=AF.Square)), reads=(CFB[c],), writes=(qb,))
                        S.op("pe", (lambda e, q=q, c=c: e.matmul(ssp[:, 0:TM], lhsT=ones_bf[:], rhs=q, start=(c == 0), stop=(c == 3))),
                             reads=(qb, CONST), writes=(ssb,))
                rsqrt(rs2[:, :], ssp[:, 0:TM], 1.0 / 512, (ssb,), RS2B)
                gvec = qng if which == 0 else kvng
                for c in range(4):
                    S.op("dve", (lambda e, c=c, which=which, gvec=gvec: e.scalar_tensor_tensor(out=cn[:, which, c, :], in0=cf[:, c, :], scalar=gvec[:, c:c + 1],
                                                                                          in1=rs2[:, :], op0=ALU.mult, op1=ALU.mult)),
                         reads=(CFB[c], RS2B, CONST), writes=(CNB[which],))
                if which == 0:
                    S.dma("sp", (lambda e, t0=t0: e.dma_start(out=cqscr[o][:, t0:t0 + TM].rearrange("(c p) t -> p c t", p=P), in_=cn[:, 0, :, :])),
                          reads=(CNB[0],), writes=(CQS,), track=CQS)
                else:
                    S.dma("sp", (lambda e, tt=tt: e.dma_start(out=kvib[(o, tt)][0:512, :].rearrange("(c p) t -> p c t", p=P), in_=cn[:, 1, :, :])),
                          reads=(CNB[1],), writes=(KVIB,), track=KVIB)
                    S.dma("sp", (lambda e, t0=t0: e.dma_start(out=latloc[o][0:512, t0:t0 + TM].rearrange("(c p) t -> p c t", p=P), in_=cn[:, 1, :, :])),
                          reads=(CNB[1],), writes=(LATL,), track=LATL)
            slv, slb = load_win(1024, 128)
            pa, pab = proj_chunk(slv, slb, h, HB, 0, 64, TM)
            pb2, pbb = proj_chunk(slv, slb, h, HB, 64, 64, TM)
            sp_, spb = sumsq(pa[0:64, 0:TM], pab, 64, TM, sqs, SQB, 0)
            rsqrt(rs2[0:64, :], sp_[0:64, 0:TM], 1.0 / 64, (spb,), RS2B)
            rope_combine(pa, pab, pb2, pbb, khgr[:, 0:1], khgr[:, 1:2], rs2, RS2B, cs, sn, TABB, ta, TAB, tb2, TBB, kr[0:64, :], KRB, TM)
            S.dma("sp", (lambda e, tt=tt: e.dma_start(out=kvib[(o, tt)][512:576, :], in_=kr[0:64, :])), reads=(KRB,), writes=(KVIB,), track=KVIB)
            S.dma("sp", (lambda e, t0=t0: e.dma_start(out=latloc[o][512:576, t0:t0 + TM], in_=kr[0:64, :])), reads=(KRB,), writes=(LATL,), track=LATL)
            S.dma("pool", (lambda e, tt=tt: e.collective_compute("AllGather", ALU.bypass, replica_groups=PAIRS,
                                                                 ins=[kvib[(o, tt)].ap().opt()], outs=[kvob[(o, tt)].ap().opt()])),
                  reads=(KVIB,), writes=(KVOB[tt],), track=KVOB[tt], inc=1)
        if next_layer(l) is not None:
            convert_layer(next_layer(l))
        if STAGE == 5:
            return
        mla_attn(l, LATL, CQS, KVOB)

    def mla_attn(l, LATL, CQS, KVOB):
        nonlocal phase_bufs
        o = l // 2
        oi = lay_odd.index(l)
        qhgn = smallc[:, 2 + o: 3 + o]
        khgn = smallc[:, 4 + o: 5 + o]
        qhgr = smallc[0:64, 72 + 2 * o: 74 + 2 * o]
        new_phase(phase_bufs)
        KT = AR.alloc([SEQ], BF16)
        KTB = [Buf("kt%d" % i) for i in range(16)]
        VT = AR.alloc([32, P], BF16)
        VTB = [Buf("vt%d" % i) for i in range(16)]
        KR = AR.alloc([SEQ], BF16)
        KRALL = Buf("krall")
        KRB2 = [KRALL for i in range(16)]
        latbufs = [(AR.alloc([4, TM], BF16), Buf("lat0")),
                   (wsl_t[:, 2, 2048:3072].rearrange("p (c t) -> p c t", c=4), Buf("lat1"))]
        cqt = AR.alloc([4, 512], BF16)
        CQTB = Buf("cqt")
        Qn = AR.alloc([2, 512], BF16)
        QNB = [Buf("qn0"), Buf("qn1")]
        Qr = AR.alloc([2, 512], BF16)
        QRB = [Buf("qr0"), Buf("qr1")]
        Et = AR.alloc([2, 512], BF16)
        ETB = [Buf("et0"), Buf("et1")]
        attn = AR.alloc([512], BF16)
        ATB = Buf("attn")
        ta = AR.alloc([512], F32)
        TAB = Buf("ta")
        tb2 = AR.alloc([512], F32)
        TBB = Buf("tb")
        cs = AR.alloc([512], F32)
        sn = AR.alloc([512], F32)
        TABB = Buf("tab")
        rs2 = AR.alloc([512], F32)
        RS2B = Buf("rs2")
        sqs = AR.alloc([2, 512], BF16)
        SQB = [Buf("sq0"), Buf("sq1")]
        trineg = AR.alloc([P], F32)
        TRNB = Buf("trineg")
        sdg = AR.alloc([P], F32)
        SDGB = Buf("sdg")
        S.dma("sp", lambda e: e.dma_start(out=trineg, in_=trineg_d[:, :]), writes=(TRNB,), track=TRNB)
        phase_bufs = KTB + VTB + [KRALL, latbufs[0][1], latbufs[1][1], CQTB, TRNB, SDGB] + QNB + QRB + ETB + [ATB, TAB, TBB, TABB, RS2B] + SQB
        uqv = uq_b[oi].rearrange("(k p) n -> p k n", p=P)
        ukvv = ukv_b[oi].rearrange("(k p) n -> p k n", p=P)
        Op, Ob = psum[6], PSB[6]
        Lp, Lb = psum[7], PSB[7]
        for hd in range(16):
            sl, slb = next_ws()
            wuq = sl[:, 0:1024].rearrange("p (k n) -> p k n", k=4)
            wukv = sl[:, 1024:2048].rearrange("p (k n) -> p k n", k=4)
            S.dma("sp", (lambda e, wuq=wuq, hd=hd: e.dma_start(out=wuq, in_=uqv[:, :, hd * 256:(hd + 1) * 256])), reads=(CV[("uq", oi)],), writes=(slb,), track=slb)
            S.dma("sp", (lambda e, wukv=wukv, hd=hd: e.dma_start(out=wukv, in_=ukvv[:, :, hd * 256:(hd + 1) * 256])), reads=(CV[("ukv", oi)],), writes=(slb,), track=slb)
            sl2, sl2b = next_ws()
            wout = sl2[:, 0:2048]
            S.dma("sp", (lambda e, wout=wout, hd=hd: e.dma_start(out=wout, in_=mlaout_b[oi][hd * 128:(hd + 1) * 128, :])), reads=(CV[("mlaout", oi)],), writes=(sl2b,), track=sl2b)
            for kt in range(16):
                if kt < 8:
                    src_lat = kvob[(o, kt)][0:512, :]
                    src_kr = kvob[(o, kt)][512:576, :]
                    srcb = KVOB[kt]
                else:
                    src_lat = latloc[o][0:512, (kt - 8) * TM:(kt - 7) * TM]
                    src_kr = latloc[o][512:576, (kt - 8) * TM:(kt - 7) * TM]
                    srcb = LATL
                lat, LATB = latbufs[kt % 2]
                S.dma("sp", (lambda e, src_lat=src_lat, lat=lat: e.dma_start(out=lat, in_=src_lat.rearrange("(c p) t -> p c t", p=P))),
                      reads=(srcb,), writes=(LATB,), track=LATB)
                if hd == 0:
                    S.dma("sp", (lambda e, src_kr=src_kr, kt=kt: e.dma_start(out=KR[0:64, kt * TM:(kt + 1) * TM], in_=src_kr)),
                          reads=(srcb,), writes=(KRB2[kt],), track=KRB2[kt])
                pk, pkb = proj_chunk(wukv, slb, lat, [LATB], 0, 128, TM, nk=4)
                kq, kqb = sqs[:, kt % 2, 0:TM], SQB[kt % 2]
                S.op("act", (lambda e, pk=pk, kq=kq: e.activation(out=kq, in_=pk[:, 0:TM], func=AF.Square)), reads=(pkb,), writes=(kqb,))
                for tc in range(2):
                    pv, pvb = next_ps()

                    def mmv(e, pv=pv, tc=tc, wukv=wukv, lat=lat):
                        ins = None
                        for kc in range(4):
                            ins = e.matmul(pv[:, 0:128], lhsT=lat[:, kc, tc * 128:(tc + 1) * 128], rhs=wukv[:, kc, 128:256], start=(kc == 0), stop=(kc == 3))
                        return ins
                    S.op("pe", mmv, reads=(LATB, slb), writes=(pvb,))
                    S.op("act", (lambda e, pv=pv, kt=kt, tc=tc: e.activation(out=VT[:, kt * 2 + tc, :], in_=pv[:, 0:128], func=AF.Copy)),
                         reads=(pvb,), writes=(VTB[kt],))
                sp_, spb = next_ps()
                S.op("pe", (lambda e, sp_=sp_, kq=kq: e.matmul(sp_[:, 0:TM], lhsT=ones_bf[:], rhs=kq, start=True, stop=True)), reads=(kqb, CONST), writes=(spb,))
                rsqrt(rs2[:, 0:TM], sp_[:, 0:TM], 1.0 / 128, (spb,), RS2B)
                S.op("dve", (lambda e, pk=pk, kt=kt: e.scalar_tensor_tensor(out=KT[:, kt * TM:(kt + 1) * TM], in0=pk[:, 0:TM], scalar=khgn, in1=rs2[:, 0:TM],
                                                                         op0=ALU.mult, op1=ALU.mult)), reads=(pkb, RS2B, CONST), writes=(KTB[kt],))
            for jt in range(4):
                q0 = jt * 512
                qs = jt % 2
                S.dma("sp", (lambda e, q0=q0: e.dma_start(out=cqt, in_=cqscr[o][:, q0:q0 + 512].rearrange("(c p) t -> p c t", p=P))),
                      reads=(CQS,), writes=(CQTB,), track=CQTB)
                S.dma("sp", (lambda e, q0=q0: e.dma_start(out=cs[0:64, :], in_=cos2_d[:, q0:q0 + 512])), writes=(TABB,), track=TABB)
                S.dma("sp", (lambda e, q0=q0: e.dma_start(out=sn[0:64, :], in_=sin2_d[:, q0:q0 + 512])), writes=(TABB,), track=TABB)
                pq, pqb = proj_chunk(wuq, slb, cqt, [CQTB], 0, 128, 512, nk=4)
                sp_, spb = sumsq(pq[:, 0:512], pqb, 128, 512, sqs, SQB, 0)
                rsqrt(rs2[:, :], sp_[:, 0:512], 1.0 / 128, (spb,), RS2B)
                S.op("dve", (lambda e, pq=pq, qs=qs: e.scalar_tensor_tensor(out=Qn[:, qs, :], in0=pq[:, 0:512], scalar=qhgn, in1=rs2[:, :], op0=ALU.mult, op1=ALU.mult)),
                     reads=(pqb, RS2B, CONST), writes=(QNB[qs],))
                pa, pab = proj_chunk(wuq, slb, cqt, [CQTB], 128, 64, 512, nk=4)
                pb2, pbb = proj_chunk(wuq, slb, cqt, [CQTB], 192, 64, 512, nk=4)
                sp_, spb = sumsq(pa[0:64, 0:512], pab, 64, 512, sqs, SQB, 1)
                rsqrt(rs2[0:64, :], sp_[0:64, 0:512], 1.0 / 64, (spb,), RS2B)
                rope_combine(pa, pab, pb2, pbb, qhgr[:, 0:1], qhgr[:, 1:2], rs2, RS2B, cs, sn, TABB, ta, TAB, tb2, TBB, Qr[0:64, qs, :], QRB[qs], 512)
                blocks = [(kb, True) for kb in range(16)] + [(kb, False) for kb in range(4 * jt + 4)]
                nb = len(blocks)
                pend = None
                for idx, (kb, remote) in enumerate(blocks):
                    kcol = kb * 128 if remote else 2048 + kb * 128
                    kti = kcol // TM
                    vch = kcol // 128
                    diag = (not remote) and kb >= 4 * jt
                    qa = (kb - 4 * jt) * 128 if diag else 0
                    ps_, psb_ = next_ps()

                    def mms(e, ps_=ps_, kcol=kcol, qa=qa, qs=qs):
                        e.matmul(ps_[:, qa:512], lhsT=KT[:, kcol:kcol + 128], rhs=Qn[:, qs, qa:512], start=True, stop=False)
                        return e.matmul(ps_[:, qa:512], lhsT=KR[0:64, kcol:kcol + 128], rhs=Qr[0:64, qs, qa:512], start=False, stop=True)
                    S.op("pe", mms, reads=(KTB[kti], KRB2[kti], QNB[qs], QRB[qs]), writes=(psb_,))
                    es = idx % 2
                    bias = rbias if remote else zeroc
                    if diag:
                        S.op("dve", (lambda e, ps_=ps_, qa=qa: e.tensor_tensor(out=sdg[:, :], in0=ps_[:, qa:qa + 128], in1=trineg[:, :], op=ALU.add)),
                             reads=(psb_, TRNB), writes=(SDGB,))
                        S.op("act", (lambda e, qa=qa, es=es: e.activation(out=Et[:, es, qa:qa + 128], in_=sdg[:, :], func=AF.Exp, bias=zeroc, scale=ATTN_SCALE)),
                             reads=(SDGB, CONST), writes=(ETB[es],))
                        if qa + 128 < 512:
                            S.op("act", (lambda e, ps_=ps_, qa=qa, es=es: e.activation(out=Et[:, es, qa + 128:512], in_=ps_[:, qa + 128:512], func=AF.Exp, bias=zeroc, scale=ATTN_SCALE)),
                                 reads=(psb_, CONST), writes=(ETB[es],))
                    else:
                        S.op("act", (lambda e, ps_=ps_, qa=qa, es=es, bias=bias: e.activation(out=Et[:, es, qa:512], in_=ps_[:, qa:512], func=AF.Exp, bias=bias, scale=ATTN_SCALE)),
                             reads=(psb_, CONST), writes=(ETB[es],))

                    def mmo(e, qa=qa, es=es, vch=vch, idx=idx, nb=nb):
                        e.matmul(Op[:, qa:512], lhsT=VT[:, vch, :], rhs=Et[:, es, qa:512], start=(idx == 0), stop=(idx == nb - 1))
                        return e.matmul(Lp[:, qa:512], lhsT=ones_bf[:], rhs=Et[:, es, qa:512], start=(idx == 0), stop=(idx == nb - 1))
                    if pend is not None:
                        S.op("pe", pend[0], reads=pend[1], writes=(Ob, Lb))
                    pend = (mmo, (VTB[kti], ETB[es], CONST))
                S.op("pe", pend[0], reads=pend[1], writes=(Ob, Lb))
                S.op("dve", (lambda e: e.reciprocal(out=ta[:, :], in_=Lp[:, 0:512])), reads=(Lb,), writes=(TAB,))
                S.op("dve", (lambda e: e.tensor_tensor(out=attn[:, :], in0=Op[:, 0:512], in1=ta[:, :], op=ALU.mult)), reads=(Ob, TAB), writes=(ATB,))
                for oc in range(NCH):
                    pt, pb = next_ps()
                    S.op("pe", (lambda e, pt=pt, oc=oc, wout=wout: e.matmul(pt[:, 0:512], lhsT=wout[:, oc * 128:(oc + 1) * 128], rhs=attn[:, :], start=True, stop=True)),
                         reads=(ATB, sl2b), writes=(pb,))
                    resid_update(l, 2, oc, pt, pb, q0, 512)

    for l in layers:
        if STAGE < 2:
            break
        if l % 2 == 0:
            even_mixer(l)
        else:
            mla_mixer(l)
        if STAGE >= 4 and STAGE not in (5, 6):
            mlp_layer(l)

    ov = outT_d.rearrange("(c p) t -> p c t", p=P)
    OUTB = Buf("out")
    for c in range(NCH):
        S.dma("sp", (lambda e, c=c: e.dma_start(out=ov[:, c, :], in_=xT[:, c, :])),
              reads=[XB[c][t] for t in range(TOK // TM)], writes=(OUTB,), track=OUTB)
    S.barrier([OUTB], ("sp",))

    with nc.Block() as block:
        S.emit_all(block)
    stack.close()
    return nc


def _rope_tables(pos):
    inv = (10000.0 ** (-np.arange(0, 64, 2, dtype=np.float32) / 64)).astype(np.float32)
    ang = pos.astype(np.float32)[:, None] * inv[None, :]
    return np.cos(ang).astype(np.float32), np.sin(ang).astype(np.float32)


def _pc(v, nchunk):
    return np.ascontiguousarray(v.reshape(nchunk, P).T)


def prepare_inputs(inp, layers):
    f = np.float32
    lay_mlp = [l for l in layers] if STAGE >= 4 else []
    lay_even = [l for l in layers if l % 2 == 0] if STAGE >= 2 else []
    lay_odd = [l for l in layers if l % 2 == 1] if STAGE >= 2 else []
    dummy = np.zeros((1, 1, 1), f)
    x = inp["x"]
    c = inp["c"]
    common = {}
    common["n1g"] = np.concatenate([_pc(inp["norm1_g"][l], 16) for l in range(4)], axis=1).astype(f)
    common["n2g"] = np.concatenate([_pc(inp["norm2_g"][l], 16) for l in range(4)], axis=1).astype(f)
    common["cT"] = np.ascontiguousarray(c.T.reshape(16, P, 4).transpose(1, 0, 2).reshape(P, 64)).astype(f)
    common["w1"] = np.ascontiguousarray(inp["mlp_w1"][lay_mlp]) if lay_mlp else dummy
    common["w2"] = np.ascontiguousarray(inp["mlp_w2"][lay_mlp]) if lay_mlp else dummy
    ev = [l // 2 for l in lay_even]
    od = [l // 2 for l in lay_odd]
    common["abin"] = np.ascontiguousarray(inp["ab_w_in"][ev]) if ev else dummy
    common["about"] = np.ascontiguousarray(inp["ab_w_out"][ev]) if ev else dummy
    common["wmT"] = np.ascontiguousarray(inp["sgu_w"].transpose(3, 0, 1, 2).reshape(P, 2 * 1024)).astype(f)
    common["sgug"] = np.ascontiguousarray(np.broadcast_to(inp["sgu_norm_g"].reshape(1, 2 * 1024), (P, 2 * 1024))).astype(f)
    common["sgub"] = np.ascontiguousarray(np.broadcast_to(inp["sgu_b"].reshape(1, 2 * 1024), (P, 2 * 1024))).astype(f)
    common["convw"] = np.ascontiguousarray(inp["conv_w"].reshape(2, 31, 8, P).transpose(3, 0, 2, 1).reshape(P, 2 * 8 * 31)).astype(f)
    common["convb"] = np.concatenate([_pc(inp["conv_b"][e], 8) for e in range(2)], axis=1).astype(f)
    common["lng"] = np.concatenate([_pc(inp["conv_ln_g"][e], 8) for e in range(2)], axis=1).astype(f)
    common["lnb"] = np.concatenate([_pc(inp["conv_ln_b"][e], 8) for e in range(2)], axis=1).astype(f)
    perm = np.concatenate([np.arange(32, 64), np.arange(0, 32)])
    win = inp["mla_w_in"]
    uq = inp["mla_w_uq"].reshape(2, 512, 16, 192)
    common["mlain"] = np.ascontiguousarray(np.concatenate([win, win[:, :, 1024 + perm]], axis=2)[od]).astype(f) if od else dummy
    common["uq"] = np.ascontiguousarray(np.concatenate([uq, uq[:, :, :, 128 + perm]], axis=3).reshape(2, 512, 4096)[od]).astype(f) if od else dummy
    common["ukv"] = np.ascontiguousarray(inp["mla_w_ukv"][od]) if od else dummy
    common["mlaout"] = np.ascontiguousarray(inp["mla_w_out"][od]) if od else dummy
    common["qng"] = np.concatenate([_pc(inp["mla_q_norm_g"][o], 4) for o in range(2)], axis=1).astype(f)
    common["kvng"] = np.concatenate([_pc(inp["mla_kv_norm_g"][o], 4) for o in range(2)], axis=1).astype(f)
    qh, kh = inp["mla_q_head_g"], inp["mla_k_head_g"]
    common["qhgn"] = np.ascontiguousarray(qh[:, :128].T).astype(f)
    common["khgn"] = np.ascontiguousarray(kh[:, :128].T).astype(f)
    common["qhgr"] = np.ascontiguousarray(np.stack([qh[0, 128:], qh[0, 128 + perm], qh[1, 128:], qh[1, 128 + perm]], axis=1)).astype(f)
    common["khgr"] = np.ascontiguousarray(np.stack([kh[0, 128:], kh[0, 128 + perm], kh[1, 128:], kh[1, 128 + perm]], axis=1)).astype(f)
    kk = np.arange(P)
    common["tri"] = (kk[:, None] <= kk[None, :]).astype(f)
    common["trineg"] = np.where(kk[:, None] <= kk[None, :], 0.0, -1.0e5).astype(f)
    maps = []
    for r in range(NCORES):
        b, half = r // 2, r % 2
        m = dict(common)
        m["xT"] = np.ascontiguousarray(x[b, half * TOK:(half + 1) * TOK, :].T)
        m["adaw"] = np.ascontiguousarray(inp["ada_w"][:, :, r * 1536:(r + 1) * 1536])
        m["adab"] = np.ascontiguousarray(np.broadcast_to(inp["ada_b"][:, None, r * 1536:(r + 1) * 1536], (4, 4, 1536))).astype(f)
        oh = np.zeros((4, 1), f)
        oh[b, 0] = 1.0
        m["onehot"] = oh
        cos, sin = _rope_tables(np.arange(half * TOK, (half + 1) * TOK))
        m["cos2"] = np.ascontiguousarray(np.concatenate([cos.T, cos.T], axis=0))
        m["sin2s"] = np.ascontiguousarray(np.concatenate([-sin.T, sin.T], axis=0))
        m["rbias"] = np.full((P, 1), 0.0 if half == 1 else -30000.0, f)
        m["cmask"] = np.full((P, 1), 1.0 if half == 1 else 0.0, f)
        maps.append(m)
    return maps


_NC_CACHE = {}


def run_layers(inp, layers):
    key = tuple(layers)
    if key not in _NC_CACHE:
        _NC_CACHE[key] = build_program(list(layers))
    nc = _NC_CACHE[key]
    maps = prepare_inputs(inp, list(layers))
    res = run_bass_kernel_spmd(nc, maps, core_ids=list(range(NCORES)))
    out = np.empty((4, SEQ, D), np.float32)
    for r in range(NCORES):
        b, half = r // 2, r % 2
        out[b, half * TOK:(half + 1) * TOK, :] = res.results[r]["outT"].T
    return out


def kernel(**inputs):
    inp = {k: np.asarray(v) for k, v in inputs.items()}
    return run_layers(inp, [0, 1, 2, 3])
```
